# Optimizing a Trainium2 kernel written in Bass

```python
import jax, jax.numpy as jnp
from jax import lax
import numpy as np

D_MODEL = 1024
BATCH = 32
SEQ = 2048
DEPTH = 2

N_A_LAYERS = DEPTH // 2
N_B_LAYERS = DEPTH - N_A_LAYERS

NORM_EPS = 1e-6

A_WINDOWS = (128, 512, 2048)
A_DILATIONS = (1, 4, 16)
A_GROUPS = 3
A_HEADS = 8
A_HEAD_DIM = 128
A_WIDTH = A_HEADS * A_HEAD_DIM
A_ROT_DIM = A_HEAD_DIM // 4
A_ROPE_THETA = 500000.0
BAND_BLOCK = 128
A_IN_WIDTH = A_GROUPS * 3 * A_WIDTH + A_WIDTH

B_HEADS = 16
B_NOPE = 64
B_ROPE = 32
B_QK_DIM = B_NOPE + B_ROPE
B_VDIM = 64
B_WIDTH = B_HEADS * B_VDIM
B_Q_LORA = 384
B_KV_LORA = 256
B_ROPE_THETA = 10000.0
B_IN_WIDTH = B_Q_LORA + B_WIDTH
ATTN_BLOCK = 128

kernel_name = "yoco_dilated_swa_mla_hybrid"


def rms_norm(x, g):
    xf = x.astype(jnp.float32)
    y = xf * lax.rsqrt(jnp.mean(xf * xf, axis=-1, keepdims=True) + NORM_EPS)
    return (y * g.astype(jnp.float32)).astype(x.dtype)


def rope(x, positions, theta):
    dim = x.shape[-1]
    half = dim // 2
    inv_freq = 1.0 / (theta ** (jnp.arange(half, dtype=jnp.float32) * (2.0 / dim)))
    ang = positions.astype(jnp.float32)[..., None] * inv_freq
    cos = jnp.cos(ang)[:, :, None, :]
    sin = jnp.sin(ang)[:, :, None, :]
    xf = x.astype(jnp.float32)
    x1, x2 = xf[..., :half], xf[..., half:]
    out = jnp.concatenate([x1 * cos - x2 * sin, x2 * cos + x1 * sin], axis=-1)
    return out.astype(x.dtype)


def partial_rope(x, positions):
    return jnp.concatenate(
        [rope(x[..., :A_ROT_DIM], positions, A_ROPE_THETA), x[..., A_ROT_DIM:]], axis=-1)


def dilated_window_attention(q, k, v, window, dilation):
    B, T, H, Dh = q.shape
    L = T // dilation
    w_sub = window // dilation
    Q = BAND_BLOCK
    nb = -(-L // Q)
    Lp = nb * Q

    def split(a):
        a = a.reshape(B, L, dilation, H, Dh).transpose(0, 2, 1, 3, 4)
        a = jnp.pad(a, ((0, 0), (0, 0), (0, Lp - L), (0, 0), (0, 0)))
        return a.reshape(B, dilation, nb, Q, H, Dh)

    def band(a):
        prev = jnp.pad(a, ((0, 0), (0, 0), (1, 0), (0, 0), (0, 0), (0, 0)))[:, :, :-1]
        return jnp.concatenate([prev, a], axis=3)

    qb = split(q)
    kb = band(split(k))
    vb = band(split(v))
    s = jnp.einsum('brnqhd,brnkhd->brnhqk', qb, kb,
                   preferred_element_type=jnp.float32) * (A_HEAD_DIM ** -0.5)
    qi = jnp.arange(Q)[:, None]
    ki = jnp.arange(2 * Q)[None, :]
    dist = Q + qi - ki
    key_pos = (jnp.arange(nb)[:, None, None] - 1) * Q + ki[None]
    allowed = (dist >= 0)[None] & (dist <= w_sub)[None] & (key_pos >= 0)
    s = jnp.where(allowed[None, None, :, None], s, -jnp.inf)
    m = jnp.max(s, axis=-1, keepdims=True)
    p = jnp.exp(s - m)
    l = jnp.sum(p, axis=-1)
    o = jnp.einsum('brnhqk,brnkhd->brnqhd', p, vb.astype(jnp.float32))
    l_q = l.transpose(0, 1, 2, 4, 3)
    o = o / l_q[..., None]
    lse = (m[..., 0] + jnp.log(l)).transpose(0, 1, 2, 4, 3)
    o = o.reshape(B, dilation, Lp, H, Dh)[:, :, :L].transpose(0, 2, 1, 3, 4).reshape(B, T, H, Dh)
    lse = lse.reshape(B, dilation, Lp, H)[:, :, :L].transpose(0, 2, 1, 3).reshape(B, T, H)
    return o, lse


def mixer_a(h, positions, w_in, w_out):
    B, T, _ = h.shape
    proj = h @ w_in
    qkv = proj[..., :A_GROUPS * 3 * A_WIDTH].reshape(B, T, A_GROUPS, 3, A_HEADS, A_HEAD_DIM)
    z = proj[..., A_GROUPS * 3 * A_WIDTH:]
    outs, lses = [], []
    for g in range(A_GROUPS):
        q = partial_rope(qkv[:, :, g, 0], positions)
        k = partial_rope(qkv[:, :, g, 1], positions)
        v = qkv[:, :, g, 2]
        o, lse = dilated_window_attention(q, k, v, A_WINDOWS[g], A_DILATIONS[g])
        outs.append(o)
        lses.append(lse)
    wts = jax.nn.softmax(jnp.stack(lses, axis=0), axis=0)
    o = jnp.einsum('gbth,gbthd->bthd', wts, jnp.stack(outs, axis=0))
    y = o.reshape(B, T, A_WIDTH).astype(h.dtype) * jax.nn.silu(z)
    return y @ w_out


def shared_latent_kv(h, positions, kv_norm, kv_w_down, kv_latent_norm, kv_w_up):
    B, T, _ = h.shape
    hn = rms_norm(h, kv_norm)
    ckr = hn @ kv_w_down
    c_kv = rms_norm(ckr[..., :B_KV_LORA], kv_latent_norm)
    k_rope = rope(ckr[..., B_KV_LORA:][:, :, None, :], positions, B_ROPE_THETA)
    kv = (c_kv @ kv_w_up).reshape(B, T, B_HEADS, B_NOPE + B_VDIM)
    k = jnp.concatenate(
        [kv[..., :B_NOPE], jnp.broadcast_to(k_rope, (B, T, B_HEADS, B_ROPE))], axis=-1)
    v = kv[..., B_NOPE:]
    return k, v


def causal_block_attention(q, k, v):
    B, T, H, Dq = q.shape
    Dv = v.shape[-1]
    nb = T // ATTN_BLOCK
    q_blocks = q.reshape(B, nb, ATTN_BLOCK, H, Dq).transpose(1, 0, 2, 3, 4)
    k_pos = jnp.arange(T)
    vf = v.astype(jnp.float32)

    def one_block(args):
        qb, idx = args
        s = jnp.einsum('bqhd,bkhd->bhqk', qb, k,
                       preferred_element_type=jnp.float32) * (B_QK_DIM ** -0.5)
        q_pos = idx * ATTN_BLOCK + jnp.arange(ATTN_BLOCK)
        mask = k_pos[None, :] <= q_pos[:, None]
        p = jax.nn.softmax(jnp.where(mask[None, None], s, -jnp.inf), axis=-1)
        return jnp.einsum('bhqk,bkhd->bqhd', p, vf)

    o = lax.map(one_block, (q_blocks, jnp.arange(nb)))
    return o.transpose(1, 0, 2, 3, 4).reshape(B, T, H, Dv)


def mixer_b(h, positions, k, v, w_in, q_norm, w_q_up, w_out):
    B, T, _ = h.shape
    proj = h @ w_in
    c_q = rms_norm(proj[..., :B_Q_LORA], q_norm)
    z = proj[..., B_Q_LORA:]
    q = (c_q @ w_q_up).reshape(B, T, B_HEADS, B_QK_DIM)
    q = jnp.concatenate([q[..., :B_NOPE], rope(q[..., B_NOPE:], positions, B_ROPE_THETA)], axis=-1)
    o = causal_block_attention(q, k, v)
    y = o.reshape(B, T, B_WIDTH).astype(h.dtype) * jax.nn.silu(z)
    return y @ w_out


def setup_inputs(seed: int = 0) -> dict:
    key = jax.random.key(seed)
    ks = jax.random.split(key, 20)
    f32 = jnp.float32

    def w(k, shape, fan_in):
        return jax.random.normal(k, shape, f32) * (fan_in ** -0.5)

    def gain(k, shape):
        return 1.0 + 0.01 * jax.random.normal(k, shape, f32)

    x = jax.random.normal(ks[0], (BATCH, SEQ, D_MODEL), f32)
    offsets = jax.random.randint(ks[1], (BATCH, 1), 0, 4096, dtype=jnp.int32)
    positions = offsets + jnp.arange(SEQ, dtype=jnp.int32)[None, :]
    return {
        "x": x,
        "positions": positions,
        "a_pre_norm": gain(ks[2], (N_A_LAYERS, D_MODEL)),
        "a_w_in": w(ks[3], (N_A_LAYERS, D_MODEL, A_IN_WIDTH), D_MODEL),
        "a_w_out": w(ks[4], (N_A_LAYERS, A_WIDTH, D_MODEL), A_WIDTH),
        "a_post_norm": gain(ks[5], (N_A_LAYERS, D_MODEL)),
        "kv_norm": gain(ks[6], (D_MODEL,)),
        "kv_w_down": w(ks[7], (D_MODEL, B_KV_LORA + B_ROPE), D_MODEL),
        "kv_latent_norm": gain(ks[8], (B_KV_LORA,)),
        "kv_w_up": w(ks[9], (B_KV_LORA, B_HEADS * (B_NOPE + B_VDIM)), B_KV_LORA),
        "b_pre_norm": gain(ks[10], (N_B_LAYERS, D_MODEL)),
        "b_w_in": w(ks[11], (N_B_LAYERS, D_MODEL, B_IN_WIDTH), D_MODEL),
        "b_q_norm": gain(ks[12], (N_B_LAYERS, B_Q_LORA)),
        "b_w_q_up": w(ks[13], (N_B_LAYERS, B_Q_LORA, B_HEADS * B_QK_DIM), B_Q_LORA),
        "b_w_out": w(ks[14], (N_B_LAYERS, B_WIDTH, D_MODEL), B_WIDTH),
        "b_post_norm": gain(ks[15], (N_B_LAYERS, D_MODEL)),
    }


def reference(x, positions, a_pre_norm, a_w_in, a_w_out, a_post_norm,
              kv_norm, kv_w_down, kv_latent_norm, kv_w_up,
              b_pre_norm, b_w_in, b_q_norm, b_w_q_up, b_w_out, b_post_norm):
    h = x
    k_shared = None
    v_shared = None
    for layer in range(DEPTH):
        if layer < N_A_LAYERS:
            i = layer
            y = mixer_a(rms_norm(h, a_pre_norm[i]), positions, a_w_in[i], a_w_out[i])
            h = h + rms_norm(y, a_post_norm[i])
        else:
            if layer == N_A_LAYERS:
                k_shared, v_shared = shared_latent_kv(
                    h, positions, kv_norm, kv_w_down, kv_latent_norm, kv_w_up)
            i = layer - N_A_LAYERS
            y = mixer_b(rms_norm(h, b_pre_norm[i]), positions, k_shared, v_shared,
                        b_w_in[i], b_q_norm[i], b_w_q_up[i], b_w_out[i])
            h = h + rms_norm(y, b_post_norm[i])
    return h
```

```python
import contextlib
import numpy as np
import concourse.bass as bass
import concourse.mybir as mybir
from concourse.bass_utils import run_bass_kernel_spmd

F32 = mybir.dt.float32
BF16 = mybir.dt.bfloat16
I32 = mybir.dt.int32
AF = mybir.ActivationFunctionType
ALU = mybir.AluOpType

NCORES = 8
T = 2048
D = 1024
NT = 16
EPS = 1e-6
DIL = (1, 4, 16)
SCALE_A = 128 ** -0.5
SCALE_B = 96 ** -0.5
TWO_PI = float(2 * np.pi)
ENGS = ['pe', 'act', 'dve', 'pool', 'sp']


class Op:
    __slots__ = ('eng', 'fn', 'deps', 'needs_inc', 'semval', 'chan', 'chanval')

    def __init__(self, eng, fn, chan=None):
        self.eng = eng
        self.fn = fn
        self.deps = []
        self.needs_inc = False
        self.semval = 0
        self.chan = chan
        self.chanval = 0


class Sched:
    def __init__(self, nc):
        self.nc = nc
        self.ops = {e: [] for e in ENGS}
        self.last_w = {}
        self.readers = {}
        self.chan_count = {}
        self.chan_last = {}

    def add(self, eng, fn, reads=(), writes=(), chan=None):
        op = Op(eng, fn, chan)
        deps = {}
        for r in reads:
            w = self.last_w.get(r)
            if w is not None:
                deps[id(w)] = (w, True)
        for wr in writes:
            w = self.last_w.get(wr)
            if w is not None and id(w) not in deps:
                deps[id(w)] = (w, False)
            for rd in self.readers.get(wr, {}).values():
                if id(rd) not in deps:
                    deps[id(rd)] = (rd, False)
        for d, raw in deps.values():
            if d is op:
                continue
            if d.chan is None and chan is None and d.eng == eng:
                if eng == 'pe' or not raw:
                    continue
            op.deps.append(d)
            if d.chan is None:
                d.needs_inc = True
        key = eng if chan is None else ('dma', chan)
        for r in reads:
            self.readers.setdefault(r, {})[key] = op
        for wr in writes:
            self.last_w[wr] = op
            self.readers[wr] = {}
        if chan is not None:
            self.chan_count[chan] = self.chan_count.get(chan, 0) + 16
            op.chanval = self.chan_count[chan]
            self.chan_last[chan] = op
        self.ops[eng].append(op)
        return op

    def barrier(self):
        lasts = [self.ops[e][-1] for e in ENGS if self.ops[e]]
        dlast = list(self.chan_last.values())
        newops = []
        for e in ENGS:
            op = Op(e, lambda eng: None)
            for d in lasts:
                if d.chan is not None:
                    continue
                if d.eng == e and e in ('pe', 'sp'):
                    continue
                op.deps.append(d)
                d.needs_inc = True
            for d in dlast:
                op.deps.append(d)
            newops.append(op)
        for op in newops:
            self.ops[op.eng].append(op)
        self.last_w.clear()
        self.readers.clear()
        self.chan_last.clear()

    def emit(self):
        nc = self.nc
        with contextlib.ExitStack() as st:
            esem = {e: st.enter_context(nc.semaphore('s_' + e)) for e in ENGS}
            csem = {c: st.enter_context(nc.semaphore('c_' + str(c))) for c in self.chan_count}
            for e in ENGS:
                cnt = 0
                for op in self.ops[e]:
                    if op.chan is None and op.needs_inc:
                        cnt += 1
                        op.semval = cnt

            def run(e, eng):
                waited = {}
                for op in self.ops[e]:
                    for d in op.deps:
                        if d.chan is not None:
                            sem, val, key = csem[d.chan], d.chanval, ('c', d.chan)
                        else:
                            sem, val, key = esem[d.eng], d.semval, ('e', d.eng)
                        if waited.get(key, 0) < val:
                            eng.wait_ge(sem, val)
                            waited[key] = val
                    ins = op.fn(eng)
                    if ins is None:
                        if op.needs_inc:
                            eng.sem_inc(esem[e], 1)
                        continue
                    if op.chan is not None:
                        ins.then_inc(csem[op.chan], 16)
                    elif op.needs_inc:
                        ins.then_inc(esem[e], 1)

            with nc.Block() as block:
                @block.tensor
                def _(eng):
                    run('pe', eng)

                @block.scalar
                def _(eng):
                    run('act', eng)

                @block.vector
                def _(eng):
                    run('dve', eng)

                @block.gpsimd
                def _(eng):
                    run('pool', eng)

                @block.sync
                def _(eng):
                    run('sp', eng)


def sl(start, count, step=1):
    return slice(start, start + (count - 1) * step + 1, step)


def build_program(NSEQ):
    nc = bass.Bass("TRN2", target_bir_lowering=False)
    S = Sched(nc)

    def dram(name, shape, dt, kind="ExternalInput"):
        return nc.dram_tensor(name, list(shape), dt, kind=kind).ap()

    x = dram("x", [NSEQ, T, D], F32)
    pos = dram("pos", [NSEQ, T], I32)
    wqkv_d = dram("wqkv", [24, 128, 8 * 384], F32)
    wz_d = dram("wz", [8, 128, 8 * 128], F32)
    wouta_d = dram("wouta", [128, 8 * 1024], F32)
    kvd_d = dram("kvd", [128, 8 * 304], F32)
    kupk_d = dram("kupk", [128, 2 * 16 * 96], F32)
    kupv_d = dram("kupv", [128, 2 * 16 * 64], F32)
    wbq_d = dram("wbq", [128, 8 * 384], F32)
    wbz_d = dram("wbz", [128, 8 * 1024], F32)
    wqu_d = dram("wqu", [128, 3 * 16 * 96], F32)
    woutb_d = dram("woutb", [128, 8 * 1024], F32)
    gains_d = dram("gains", [128, 40], F32)
    gpost_d = dram("gpost", [2, 1024], F32)
    cst_d = dram("cst", [128, 768], F32)
    out = dram("out", [NSEQ, T, D], F32, kind="ExternalOutput")

    BASE = 16512
    KB = 1024

    def sb(name, shape, dt, off):
        assert off % 32 == 0, (name, off)
        nbytes = int(np.prod(shape[1:])) * (2 if dt == BF16 else 4)
        assert off + nbytes <= 229344 - BASE, (name, off, nbytes)
        return nc.alloc_sbuf_tensor_at(name, list(shape), dt, offset=BASE + off)

    class Bump:
        def __init__(self, start, end):
            self.p = start
            self.end = end

        def __call__(self, name, shape, dt):
            nbytes = int(np.prod(shape[1:])) * (2 if dt == BF16 else 4)
            nbytes = (nbytes + 63) // 64 * 64
            t = sb(name, shape, dt, self.p)
            self.p += nbytes
            assert self.p <= self.end, (name, self.p, self.end)
            return t

    pb = Bump(0, 10 * KB)
    cstb = pb("cstb", [128, 768], BF16)
    gains = pb("gains", [128, 40], F32)
    gpost = pb("gpost", [128, 2, 1024], F32)
    stat = pb("stat", [128, 64], F32)
    rstdb = pb("rstdb", [128, 16], F32)
    ident = cstb[:, 0:128]
    maskA = cstb[:, 128:384]
    maskcur = cstb[:, 256:384]
    ones = cstb[:, 384:512]
    mask01 = cstb[:, 512:768]

    H1_OFF = 10 * KB
    YT_OFF = 74 * KB
    TAB_OFF = 106 * KB
    LOC_OFF = 122 * KB
    LOC_END = 229344 - BASE

    ps = [nc.alloc_psum_tensor("ps%d" % i, [128, 512], F32) for i in range(8)]
    psb = [p.bitcast(BF16) for p in ps]

    tabC = sb("tabC", [128, T], F32, TAB_OFF)
    tabS = sb("tabS", [128, T], F32, TAB_OFF + 8 * KB)

    def mm(out_, lhsT, rhs, start, stop, reads, writes):
        S.add('pe', lambda e: e.matmul(out_, lhsT, rhs, start=start, stop=stop), reads, writes)

    def tr(out_, in_, reads, writes):
        S.add('pe', lambda e: e.transpose(out_, in_, ident), reads, writes)

    def act(out_, in_, func, reads, writes, scale=1.0, bias=0.0, accum=None):
        S.add('act', lambda e: e.activation(out=out_, in_=in_, func=func, bias=bias, scale=scale,
                                            accum_out=accum), reads, writes)

    def tt(eng, out_, in0, in1, op, reads, writes):
        S.add(eng, lambda e: e.tensor_tensor(out=out_, in0=in0, in1=in1, op=op), reads, writes)

    def ts(eng, out_, in0, s1, s2, op0, op1, reads, writes):
        if s2 is None:
            S.add(eng, lambda e: e.tensor_scalar(out=out_, in0=in0, scalar1=s1, scalar2=None, op0=op0),
                  reads, writes)
        else:
            S.add(eng, lambda e: e.tensor_scalar(out=out_, in0=in0, scalar1=s1, scalar2=s2, op0=op0, op1=op1),
                  reads, writes)

    def stt(eng, out_, in0, scalar, in1, op0, op1, reads, writes):
        S.add(eng, lambda e: e.scalar_tensor_tensor(out=out_, in0=in0, scalar=scalar, in1=in1, op0=op0, op1=op1),
              reads, writes)

    def cp(eng, out_, in_, reads, writes):
        if eng == 'act':
            S.add('act', lambda e: e.copy(out=out_, in_=in_), reads, writes)
        else:
            S.add(eng, lambda e: e.tensor_copy(out=out_, in_=in_), reads, writes)

    def recip(out_, in_, reads, writes):
        S.add('dve', lambda e: e.reciprocal(out=out_, in_=in_), reads, writes)

    def memset(eng, out_, val, writes):
        S.add(eng, lambda e: e.memset(out_, val), (), writes)

    def dma(q, out_, in_, chan, reads, writes):
        S.add(q, lambda e: e.dma_start(out=out_, in_=in_), reads, writes, chan=chan)

    class Rot:
        def __init__(self, banks):
            self.banks = banks
            self.i = 0

        def __call__(self):
            b = self.banks[self.i % len(self.banks)]
            self.i += 1
            return b

    dma('pool', cstb[:], cst_d, 'cst', (), ['cst'])
    dma('sp', gains[:], gains_d, 'gains', (), ['gains'])
    dma('sp', gpost[:, 0, :], gpost_d[0, :].partition_broadcast(128), 'gpost0', (), ['gpost0'])
    dma('sp', gpost[:, 1, :], gpost_d[1, :].partition_broadcast(128), 'gpost1', (), ['gpost1'])
    S.barrier()

    def rsqrt_col(col_out, col_in, n, tag):
        act(col_out, col_in, AF.Sqrt, [tag + 'in'], [tag + 'sq', tag + 'rs'], scale=1.0 / n, bias=EPS)
        recip(col_out, col_out, [tag + 'sq'], [tag + 'rs'])

    def tab_tmps(loc):
        return (loc("posi", [128, 512], I32), loc("posf", [128, 512], F32), loc("tki", [128, 512], I32),
                loc("tkf", [128, 512], F32))

    def tab_prep(s, fcol, ch, tmps):
        posi, posf, ki, kf = tmps
        cols = slice(ch * 512, (ch + 1) * 512)
        dma('sp', posi[:], pos[s, cols].partition_broadcast(128), 'posi', (), ['posi'])
        cp('dve', posf[:], posi[:], ['posi'], ['posf'])
        for tab, tn, phcol in ((tabC, 'tabC', 30), (tabS, 'tabS', 31)):
            u = tab[:, cols]
            ts('dve', u, posf[:], gains[:, fcol:fcol + 1], gains[:, phcol:phcol + 1],
               ALU.mult, ALU.add, ['posf'], [tn])
            cp('dve', ki[:], u, [tn], ['tki'])
            cp('dve', kf[:], ki[:], ['tki'], ['tkf'])
            tt('dve', u, u, kf[:], ALU.subtract, [tn, 'tkf'], [tn])
            ts('dve', kf[:], u, 0.5, None, ALU.is_gt, None, [tn], ['tkf'])
            tt('dve', u, u, kf[:], ALU.subtract, [tn, 'tkf'], [tn])

    def tab_sin():
        for tab, tn in ((tabC, 'tabC'), (tabS, 'tabS')):
            act(tab[:, :], tab[:, :], AF.Sin, [tn], [tn], scale=TWO_PI)

    ropetmp = {}

    def rope(dst, dname, bank, tc, d, rows, t1s, t2s, k, stg):
        stgA, stgB = stg
        p = ps[bank]
        nl = 512 // d
        ns = len(t1s)
        t1 = t1s[k % ns]
        t2 = t2s[k % ns]
        ka, kb_ = k % len(stgA), k % len(stgB)
        sa, sb2 = stgA[ka], stgB[kb_]
        cols = slice(tc * 512, (tc + 1) * 512)

        def dv(p0, p1):
            if d == 1:
                return dst[p0:p1, cols]
            return dst[p0:p1, :].rearrange("p (r l) -> p r l", r=d)[:, :, tc * nl:(tc + 1) * nl]

        def sv(ap):
            if d == 1:
                return ap
            return ap.rearrange("p (l r) -> p r l", r=d)

        bn = 'ps%d' % bank
        n1, n2 = 't1_%d' % (k % ns), 't2_%d' % (k % ns)
        san, sbn = 'stgA%d' % ka, 'stgB%d' % kb_
        cp('act', sa[0:64, :], p[0:64, 0:512], [bn], [san])
        if rows > 64:
            cp('act', dv(64, rows), sv(p[64:rows, 0:512]), [bn], [dname + 'hi'])
        dma('sp', sb2[0:32, :], sa[32:64, :], sbn, [san], [sbn + 'a'])
        dma('sp', sb2[32:64, :], sa[0:32, :], sbn, [san], [sbn + 'b'])
        tt('dve', t1[0:64, :], sa[0:64, :], tabC[0:64, cols], ALU.mult, [san], [n1])
        tt('dve', t2[0:64, :], sb2[0:64, :], tabS[0:64, cols], ALU.mult, [sbn + 'a', sbn + 'b'], [n2])
        tt('dve', dv(0, 64), sv(t1[0:64, :]), sv(t2[0:64, :]), ALU.add, [n1, n2], [dname + 'lo'])

    def pipeline(stages, n, newest_first=False):
        ns = len(stages)
        for i in range(n + ns - 1):
            ks = range(ns) if newest_first else reversed(range(ns))
            for k in ks:
                t_ = i - k
                if 0 <= t_ < n:
                    stages[k](t_)

    def rope_pair(dq, dk, qname, kname, bq, bk, tc, d, t1s, t2s, k, stg):
        stgA, stgB = stg
        nl = 512 // d
        ns = len(t1s)
        t1 = t1s[k % ns]
        t2 = t2s[k % ns]
        ka, kb_ = k % len(stgA), k % len(stgB)
        sa, sb2 = stgA[ka], stgB[kb_]
        cols = slice(tc * 512, (tc + 1) * 512)

        def dv(dst, p0, p1):
            if d == 1:
                return dst[p0:p1, cols]
            return dst[p0:p1, :].rearrange("p (r l) -> p r l", r=d)[:, :, tc * nl:(tc + 1) * nl]

        def sv(ap):
            if d == 1:
                return ap
            return ap.rearrange("p (l r) -> p r l", r=d)

        n1, n2 = 't1_%d' % (k % ns), 't2_%d' % (k % ns)
        san, sbn = 'stgA%d' % ka, 'stgB%d' % kb_
        bqn, bkn = 'ps%d' % bq, 'ps%d' % bk
        cp('act', sa[0:64, :], ps[bq][0:64, 0:512], [bqn], [san + 'q'])
        cp('act', sa[64:128, :], ps[bk][0:64, 0:512], [bkn], [san + 'k'])
        cp('act', dv(dq, 64, 128), sv(ps[bq][64:128, 0:512]), [bqn], [qname + 'hi'])
        cp('act', dv(dk, 64, 128), sv(ps[bk][64:128, 0:512]), [bkn], [kname + 'hi'])
        dma('sp', sb2[0:32, :], sa[32:64, :], sbn, [san + 'q'], [sbn + 'a'])
        dma('sp', sb2[32:64, :], sa[0:32, :], sbn, [san + 'q'], [sbn + 'b'])
        dma('sp', sb2[64:96, :], sa[96:128, :], sbn, [san + 'k'], [sbn + 'c'])
        dma('sp', sb2[96:128, :], sa[64:96, :], sbn, [san + 'k'], [sbn + 'd'])
        tt('dve', t1[:, :], sa[:, :], tabC[:, cols], ALU.mult, [san + 'q', san + 'k'], [n1])
        tt('dve', t2[:, :], sb2[:, :], tabS[:, cols], ALU.mult, [sbn + x for x in 'abcd'], [n2])
        tt('dve', dv(dq, 0, 64), sv(t1[0:64, :]), sv(t2[0:64, :]), ALU.add, [n1, n2], [qname + 'lo'])
        tt('dve', dv(dk, 0, 64), sv(t1[64:128, :]), sv(t2[64:128, :]), ALU.add, [n1, n2], [kname + 'lo'])

    for s in range(NSEQ):
        hnT = sb("hnT", [128, 8, T], BF16, H1_OFF)
        loc = Bump(LOC_OFF, LOC_END)
        if s == 0:
            tmps = tab_tmps(loc)
            for ch in range(4):
                tab_prep(s, 29, ch, tmps)
            tab_sin()
        xt = [loc("xt%d" % i, [128, D], F32) for i in range(3)]
        hnb = [loc("hnb%d" % i, [128, D], BF16) for i in range(2)]
        def p1_a(t_):
            k3, k2 = t_ % 3, t_ % 2
            dma('sp', xt[k3][:], x[s, t_ * 128:(t_ + 1) * 128, :], 'xt%d' % k3, (), ['xt%d' % k3])
            act(hnb[k2][:], xt[k3][:], AF.Square, ['xt%d' % k3], ['hnb%d' % k2, 'ssqin'], accum=stat[:, 0:1])
            rsqrt_col(stat[:, 8 + k2:9 + k2], stat[:, 0:1], D, 'ssq')
            act(hnb[k2][:], xt[k3][:], AF.Copy, ['xt%d' % k3, 'ssqrs'], ['hnb%d' % k2],
                scale=stat[:, 8 + k2:9 + k2])

        def p1_b(t_):
            k2 = t_ % 2
            b = k2
            for c in range(8):
                tr(psb[b][:, c * 128:(c + 1) * 128], hnb[k2][:, c * 128:(c + 1) * 128],
                   ['hnb%d' % k2], ['ps%d' % b])
            tt('dve', hnT[:, :, t_ * 128:(t_ + 1) * 128],
               psb[b][:, 0:1024].rearrange("p (c t) -> p c t", c=8),
               gains[:, 0:8].unsqueeze(2).broadcast_to([128, 8, 128]), ALU.mult,
               ['ps%d' % b], ['hnT'])

        pipeline([p1_a, p1_b], NT, newest_first=True)
        S.barrier()

        yT = sb("yT", [128, 8, T], BF16, YT_OFF)
        hb = Bump(H1_OFF + 32 * KB, H1_OFF + 64 * KB)
        qT = [hb("qT%d" % i, [128, T], BF16) for i in range(2)]
        kT = [hb("kT%d" % i, [128, T], BF16) for i in range(2)]
        vv = [hb("v%d" % i, [128, 16, 128], BF16) for i in range(2)]
        gz = [hb("gz%d" % i, [128, T], BF16) for i in range(2)]
        loc = Bump(LOC_OFF, LOC_END)
        OL = [loc("OL%d" % i, [128, 2, T], F32) for i in range(2)]
        wq = [loc("wqkv%d" % i, [128, 8, 384], BF16) for i in range(2)]
        wzs = [loc("wz%d" % i, [128, 8, 128], BF16) for i in range(2)]
        th = [loc("th%d" % i, [128, 512], F32) for i in range(2)]
        t1s = [loc("t1_%d" % i, [128, 512], F32) for i in range(2)]
        t2s = [loc("t2_%d" % i, [128, 512], F32) for i in range(2)]
        stg = ([loc("stgA%d" % i, [128, 512], F32) for i in range(3)],
               [loc("stgB%d" % i, [128, 512], F32) for i in range(3)])
        ostg = [loc("ostg%d" % i, [128, 2, 128], F32) for i in range(4)]
        pT = [loc("pT%d" % i, [128, 256], BF16) for i in range(4)]
        pS = [loc("pS%d" % i, [128, 128], BF16) for i in range(4)]
        pj = Rot([0, 1, 2, 7])
        STB = [3, 4]
        OVB = [5, 6]
        cnt = {'rope': 0, 'th': 0}

        def load_unit(u):
            dma('pool', wq[u % 2][:].rearrange("p c n -> p (c n)"), wqkv_d[u], 'wqkv%d' % (u % 2),
                (), ['wqkv%d' % (u % 2)])

        def load_wz(hs):
            dma('pool', wzs[hs % 2][:].rearrange("p c n -> p (c n)"), wz_d[hs], 'wz%d' % (hs % 2),
                (), ['wz%d' % (hs % 2)])

        def gate_item(hs, tc):
            hk = hs % 2
            b = pj()
            cols = slice(tc * 512, (tc + 1) * 512)
            for c in range(8):
                mm(ps[b][:, 0:512], wzs[hk][:, c, :], hnT[:, c, cols], c == 0, c == 7,
                   ['wz%d' % hk], ['ps%d' % b])
            tk = cnt['th'] % 2
            cnt['th'] += 1
            act(th[tk][:], ps[b][:, 0:512], AF.Tanh, ['ps%d' % b], ['th%d' % tk], scale=0.5)
            stt('dve', gz[hk][:, cols], th[tk][:], 1.0, ps[b][:, 0:512], ALU.add, ALU.mult,
                ['th%d' % tk, 'ps%d' % b], ['gz%d' % hk])

        def qk_item(u, tc):
            k = u % 2
            d = DIL[u % 3]
            bq, bk = pj(), pj()
            cols = slice(tc * 512, (tc + 1) * 512)
            for j, b in ((0, bq), (1, bk)):
                for c in range(8):
                    mm(ps[b][:, 0:512], wq[k][:, c, j * 128:(j + 1) * 128], hnT[:, c, cols],
                       c == 0, c == 7, ['wqkv%d' % k], ['ps%d' % b])
            rope_pair(qT[k], kT[k], 'qT%d' % k, 'kT%d' % k, bq, bk, tc, d, t1s, t2s, cnt['rope'], stg)
            cnt['rope'] += 1

        def v_item(u, jb):
            k = u % 2
            d = DIL[u % 3]
            nb = (T // d) // 128
            b = pj()
            for jj in range(4):
                j = jb + jj
                r, n = j // nb, j % nb
                toks = sl(n * 128 * d + r, 128, d)
                for c in range(8):
                    mm(ps[b][:, jj * 128:(jj + 1) * 128], hnT[:, c, toks], wq[k][:, c, 256:384],
                       c == 0, c == 7, ['wqkv%d' % k], ['ps%d' % b])
            cp('act', vv[k][:, jb:jb + 4, :], ps[b][:, 0:512].rearrange("p (a b) -> p a b", a=4),
               ['ps%d' % b], ['v%d' % k])

        def proj_items(u):
            its = []
            for tc in range(4):
                its.append((qk_item, (u, tc)))
                its.append((v_item, (u, tc * 4)))
            return its

        def att_qk(u, j):
            k = u % 2
            nb = (T // DIL[u % 3]) // 128
            n = j % nb
            gi = u * 16 + j
            sb_ = STB[gi % 2]
            sn = 'ps%d' % sb_
            qn = ['qT%dlo' % k, 'qT%dhi' % k, 'kT%dlo' % k, 'kT%dhi' % k]
            qcols = slice(j * 128, (j + 1) * 128)
            c0 = 0 if n > 0 else 128
            if n > 0:
                mm(ps[sb_][:, 0:128], kT[k][:, (j - 1) * 128:j * 128], qT[k][:, qcols], True, True, qn, [sn])
            mm(ps[sb_][:, 128:256], kT[k][:, qcols], qT[k][:, qcols], True, True, qn, [sn])

        def att_exp(u, j):
            nb = (T // DIL[u % 3]) // 128
            n = j % nb
            gi = u * 16 + j
            sb_ = STB[gi % 2]
            sn = 'ps%d' % sb_
            c0 = 0 if n > 0 else 128
            pk = gi % 4
            pn = 'pT%d' % pk
            act(pT[pk][:, c0:256], ps[sb_][:, c0:256], AF.Exp, [sn], [pn], scale=SCALE_A)
            tt('pool', pT[pk][:, c0:256], pT[pk][:, c0:256], mask01[:, c0:256], ALU.mult, [pn], [pn])
            if n > 0:
                tt('pool', pS[pk][:], pT[pk][:, 0:128], pT[pk][:, 128:256], ALU.add, [pn], ['pS%d' % pk])

        def att_pv(u, j):
            k = u % 2
            hs, g = divmod(u, 3)
            hk = hs % 2
            d = DIL[g]
            nb = (T // d) // 128
            r, n = j // nb, j % nb
            gi = u * 16 + j
            pk = gi % 4
            pn = 'pT%d' % pk
            ob_ = OVB[gi % 2]
            on = 'ps%d' % ob_
            vn = 'v%d' % k
            if n > 0:
                mm(ps[ob_][:, 0:128], vv[k][:, j - 1, :], pT[pk][:, 0:128], True, False, [pn, vn], [on])
            mm(ps[ob_][:, 0:128], vv[k][:, j, :], pT[pk][:, 128:256], n == 0, True, [pn, vn], [on])
            if n > 0:
                mm(ps[ob_][:, 128:256], ones, pS[pk][:], True, True, ['pS%d' % pk], [on])
            else:
                mm(ps[ob_][:, 128:256], ones, pT[pk][:, 128:256], True, True, [pn], [on])
            toks = sl(n * 128 * d + r, 128, d)
            OLv = OL[hk][:, :, toks]
            src = ps[ob_][:, 0:256].rearrange("p (a b) -> p a b", a=2)
            if g == 0:
                quarters = [j // 4]
            elif g == 1:
                quarters = [n]
            else:
                quarters = [0, 1, 2, 3]
            oln = ['OL%d_%d' % (hk, q_) for q_ in quarters]
            if g == 0:
                cp('act', OLv, src, [on], oln)
            else:
                ok_ = gi % 4
                cp('act', ostg[ok_][:], src, [on], ['ostg%d' % ok_])
                tt('dve', OLv, ostg[ok_][:], OLv, ALU.add, ['ostg%d' % ok_] + oln, oln)

        def finalize_q(hs, q_):
            hk = hs % 2
            cols = slice(q_ * 512, (q_ + 1) * 512)
            on_ = ['OL%d_%d' % (hk, q_)]
            recip(OL[hk][:, 1, cols], OL[hk][:, 1, cols], on_, on_)
            tt('dve', OL[hk][:, 0, cols], OL[hk][:, 0, cols], OL[hk][:, 1, cols], ALU.mult, on_, on_)
            stt('dve', yT[:, hs, cols], OL[hk][:, 0, cols], 0.5, gz[hk][:, cols], ALU.mult, ALU.mult,
                on_ + ['gz%d' % hk], ['yT'])

        load_wz(0)
        load_unit(0)
        load_unit(1)
        for tc in range(4):
            gate_item(0, tc)
        for f, a in proj_items(0):
            f(*a)
        for u in range(24):
            hs, g = divmod(u, 3)
            nxt = []
            if u + 1 < 24:
                if (u + 1) % 3 == 0:
                    nxt += [(gate_item, (hs + 1, tc)) for tc in range(4)]
                nxt += proj_items(u + 1)
            if u + 2 < 24:
                load_unit(u + 2)
            if g == 0 and hs + 1 < 8:
                load_wz(hs + 1)
            att_qk(u, 0)
            att_exp(u, 0)
            att_qk(u, 1)
            ni = 0
            for j in range(16):
                if j + 2 < 16:
                    att_qk(u, j + 2)
                if j + 1 < 16:
                    att_exp(u, j + 1)
                att_pv(u, j)
                target = min(len(nxt), (len(nxt) * (j + 1) + 9) // 10)
                while ni < target:
                    f, a = nxt[ni]
                    f(*a)
                    ni += 1
                if g == 0 and hs > 0 and j in (3, 6, 9, 12):
                    finalize_q(hs - 1, (j - 3) // 3)
        for q_ in range(4):
            finalize_q(7, q_)
        S.barrier()

        h1 = sb("h1", [128, NT, D], F32, H1_OFF)
        loc = Bump(LOC_OFF, LOC_END)
        wouta = loc("wouta", [128, 8, 1024], BF16)
        junk = [loc("junk%d" % i, [128, 512], BF16) for i in range(2)]
        tmpf = [loc("tmpf%d" % i, [128, D], F32) for i in range(2)]
        for c_ in range(8):
            dma('pool', wouta[:, c_, :], wouta_d[:, c_ * 1024:(c_ + 1) * 1024], 'wout%d' % c_, (), ['wout%d' % c_])
        YB = [[0, 1], [2, 3]]

        def pn_mm(t_, yTt, wout_):
            b0, b1 = YB[t_ % 2]
            for hf, b in ((0, b0), (1, b1)):
                for c in range(8):
                    mm(ps[b][:, 0:512], yTt(c), wout_[:, c, hf * 512:(hf + 1) * 512], c == 0, c == 7,
                       ['wout%d' % c, 'yTt%d' % (t_ % 2)], ['ps%d' % b])

        def pn_fin(t_, gp, resid_tile, dst_tile, dst_name, resid_reads):
            k2 = t_ % 2
            b0, b1 = YB[k2]
            c0_ = 16 + 4 * k2
            act(junk[0][:], ps[b0][:, 0:512], AF.Square, ['ps%d' % b0], ['junk0', 'pq0'], accum=stat[:, c0_:c0_ + 1])
            act(junk[1][:], ps[b1][:, 0:512], AF.Square, ['ps%d' % b1], ['junk1', 'pq1'],
                accum=stat[:, c0_ + 1:c0_ + 2])
            tt('dve', stat[:, c0_ + 2:c0_ + 3], stat[:, c0_:c0_ + 1], stat[:, c0_ + 1:c0_ + 2], ALU.add,
               ['pq0', 'pq1'], ['pnin'])
            rsqrt_col(stat[:, c0_ + 3:c0_ + 4], stat[:, c0_ + 2:c0_ + 3], D, 'pn')
            for hf, b in ((0, b0), (1, b1)):
                stt('dve', tmpf[k2][:, hf * 512:(hf + 1) * 512], ps[b][:, 0:512], stat[:, c0_ + 3:c0_ + 4],
                    gp[:, hf * 512:(hf + 1) * 512], ALU.mult, ALU.mult, ['ps%d' % b, 'pnrs'], ['tmpf%d' % k2])
            tt('pool', dst_tile, resid_tile, tmpf[k2][:], ALU.add, ['tmpf%d' % k2] + resid_reads, [dst_name])

        def p3a_a(t_):
            dma('sp', h1[:, t_, :], x[s, t_ * 128:(t_ + 1) * 128, :], 'h1_%d' % t_, (), ['h1_%d' % t_])
            pn_mm(t_, lambda c: yT[:, c, t_ * 128:(t_ + 1) * 128], wouta)

        def p3a_b(t_):
            pn_fin(t_, gpost[:, 0, :], h1[:, t_, :], h1[:, t_, :], 'h1_%d' % t_, ['h1_%d' % t_])

        tmps = tab_tmps(loc)

        def p3a_t(t_):
            if t_ % 4 == 1:
                tab_prep(s, 32, t_ // 4, tmps)

        pipeline([p3a_a, p3a_b, p3a_t], NT)
        tab_sin()
        S.barrier()

        loc = Bump(LOC_OFF, LOC_END)
        cqT = loc("cqT", [128, 3, T], BF16)
        ckvT = loc("ckvT", [128, 2, T], BF16)
        krT = loc("krT", [64, T], BF16)
        wqu = loc("wqu", [128, 3, 16, 96], BF16)
        kupk = loc("kupk", [128, 2, 16, 96], BF16)
        kupv = loc("kupv", [128, 2, 16, 64], BF16)
        p4_start = loc.p
        yb = Bump(YT_OFF, YT_OFF + 32 * KB)
        hnb = [yb("hnbB%d" % i, [128, D], BF16) for i in range(2)]
        hTb = [loc("hTb%d" % i, [128, 8, 512], BF16) for i in range(2)]
        hTkv = [loc("hTkv%d" % i, [128, 8, 512], BF16) for i in range(2)]
        kvd = yb("kvd", [128, 8, 304], BF16)
        wbq = yb("wbq", [128, 8, 384], BF16)
        sq = yb("sq", [128, 3, 512], BF16)
        rsf = yb("rsf", [128, 512], F32)
        t1s = [yb("t1b%d" % i, [64, 512], F32) for i in range(2)]
        t2s = [yb("t2b%d" % i, [64, 512], F32) for i in range(2)]
        dma('pool', wqu[:].rearrange("p a b c -> p (a b c)"), wqu_d, 'wqu', (), ['wqu'])
        dma('pool', kupk[:].rearrange("p a b c -> p (a b c)"), kupk_d, 'kupk', (), ['kupk'])
        dma('pool', kupv[:].rearrange("p a b c -> p (a b c)"), kupv_d, 'kupv', (), ['kupv'])
        dma('pool', kvd[:].rearrange("p c n -> p (c n)"), kvd_d, 'kvd', (), ['kvd'])
        dma('pool', wbq[:].rearrange("p c n -> p (c n)"), wbq_d, 'wbq', (), ['wbq'])
        for k_ in range(2):
            memset('pool', t2s[k_][:], 0.0, ['t2_%da' % k_, 't2_%db' % k_])
        def p3b_ta(t_):
            k2 = t_ % 2
            act(hnb[k2][:], h1[:, t_, :], AF.Square, [], ['hnb%d' % k2, 'ssqin'], accum=stat[:, 0:1])
            rsqrt_col(rstdb[:, t_:t_ + 1], stat[:, 0:1], D, 'ssq')
            act(hnb[k2][:], h1[:, t_, :], AF.Copy, ['ssqrs'], ['hnb%d' % k2], scale=rstdb[:, t_:t_ + 1])

        def p3b_tb(t_):
            k2 = t_ % 2
            tg, tl = divmod(t_, 4)
            g2 = tg % 2
            b = 0
            for c in range(8):
                tr(psb[b][:, c * 128:(c + 1) * 128], hnb[k2][:, c * 128:(c + 1) * 128],
                   ['hnb%d' % k2], ['ps%d' % b])
            srcv = psb[b][:, 0:1024].rearrange("p (c t) -> p c t", c=8)
            tt('dve', hTb[g2][:, :, tl * 128:(tl + 1) * 128], srcv,
               gains[:, 16:24].unsqueeze(2).broadcast_to([128, 8, 128]), ALU.mult, ['ps%d' % b], ['hTb%d' % g2])
            tt('dve', hTkv[g2][:, :, tl * 128:(tl + 1) * 128], srcv,
               gains[:, 8:16].unsqueeze(2).broadcast_to([128, 8, 128]), ALU.mult, ['ps%d' % b], ['hTkv%d' % g2])

        def g0(tg):
            g2 = tg % 2
            for j in range(2):
                for c in range(8):
                    mm(ps[1 + j][:, 0:512], kvd[:, c, j * 128:(j + 1) * 128], hTkv[g2][:, c, :], c == 0, c == 7,
                       ['kvd', 'hTkv%d' % g2], ['ps%d' % (1 + j)])
            for c in range(8):
                mm(ps[3][0:48, 0:512], kvd[:, c, 256:304], hTkv[g2][:, c, :], c == 0, c == 7,
                   ['kvd', 'hTkv%d' % g2], ['ps3'])
            for j in range(2):
                act(sq[:, j, :], ps[1 + j][:, 0:512], AF.Square, ['ps%d' % (1 + j)], ['sq%d' % j])

        def g1(tg):
            gcols = slice(tg * 512, (tg + 1) * 512)
            for j in range(2):
                mm(ps[4][:, 0:512], ones, sq[:, j, :], j == 0, j == 1, ['sq%d' % j], ['ps4'])
            act(rsf[:], ps[4][:, 0:512], AF.Sqrt, ['ps4'], ['rsf'], scale=1.0 / 256, bias=EPS)
            recip(rsf[:], rsf[:], ['rsf'], ['rsf'])
            for j in range(2):
                stt('dve', ckvT[:, j, gcols], ps[1 + j][:, 0:512], gains[:, 24 + j:25 + j], rsf[:],
                    ALU.mult, ALU.mult, ['ps%d' % (1 + j), 'rsf'], ['ckvT'])
            kk = tg % 2
            tt('dve', t1s[kk][0:48, :], ps[3][0:48, 0:512], tabC[0:48, gcols], ALU.mult, ['ps3'], ['t1_%d' % kk])
            tt('dve', t2s[kk][0:16, :], ps[3][32:48, 0:512], tabS[0:16, gcols], ALU.mult, ['ps3'], ['t2_%da' % kk])
            tt('dve', t2s[kk][32:48, :], ps[3][0:16, 0:512], tabS[32:48, gcols], ALU.mult, ['ps3'], ['t2_%db' % kk])
            tt('pool', krT[0:48, gcols], t1s[kk][0:48, :], t2s[kk][0:48, :], ALU.add,
               ['t1_%d' % kk, 't2_%da' % kk, 't2_%db' % kk], ['krT'])

        def g2_(tg):
            g2 = tg % 2
            for j in range(3):
                for c in range(8):
                    mm(ps[5 + j][:, 0:512], wbq[:, c, j * 128:(j + 1) * 128], hTb[g2][:, c, :], c == 0, c == 7,
                       ['wbq', 'hTb%d' % g2], ['ps%d' % (5 + j)])
                act(sq[:, j, :], ps[5 + j][:, 0:512], AF.Square, ['ps%d' % (5 + j)], ['sq%d' % j])

        def g3(tg):
            gcols = slice(tg * 512, (tg + 1) * 512)
            for j in range(3):
                mm(ps[4][:, 0:512], ones, sq[:, j, :], j == 0, j == 2, ['sq%d' % j], ['ps4'])
            act(rsf[:], ps[4][:, 0:512], AF.Sqrt, ['ps4'], ['rsf'], scale=1.0 / 384, bias=EPS)
            recip(rsf[:], rsf[:], ['rsf'], ['rsf'])
            for j in range(3):
                stt('dve', cqT[:, j, gcols], ps[5 + j][:, 0:512], gains[:, 26 + j:27 + j], rsf[:],
                    ALU.mult, ALU.mult, ['ps%d' % (5 + j), 'rsf'], ['cqT'])

        GP = [g0, g1, g2_, g3]
        p3b_ta(0)
        for t_ in range(NT):
            if t_ + 1 < NT:
                p3b_ta(t_ + 1)
            p3b_tb(t_)
            tg, tl = divmod(t_, 4)
            if tg >= 1:
                GP[tl](tg - 1)
        for f in GP:
            f(3)
        S.barrier()

        obuf = sb("obuf", [128, NT, D], BF16, YT_OFF)
        loc = Bump(p4_start, LOC_END)
        qTb = [loc("qTb%d" % i, [128, T], BF16) for i in range(2)]
        kTb = [loc("kTb%d" % i, [128, T], BF16) for i in range(2)]
        vb = [loc("vb%d" % i, [128, 16, 66], BF16) for i in range(2)]
        pTb = [loc("pTb%d" % i, [128, 512], BF16) for i in range(4)]
        t1s = [loc("t1c%d" % i, [64, 512], F32) for i in range(2)]
        t2s = [loc("t2c%d" % i, [64, 512], F32) for i in range(2)]
        stg = ([loc("stgAc%d" % i, [64, 512], F32) for i in range(2)],
               [loc("stgBc%d" % i, [64, 512], F32) for i in range(2)])
        rl = [loc("rl%d" % i, [128, 4], F32) for i in range(2)]
        for k_ in range(2):
            memset('pool', vb[k_][:, :, 64:65], 1.0, ['vb%d' % k_])
        pj = Rot([0, 1])
        STB = [2, 3, 4, 7]
        OVB = [5, 6]
        cnt = {'rope': 0}

        def bq_item(h, tc):
            k = h % 2
            b = pj()
            cols = slice(tc * 512, (tc + 1) * 512)
            for c in range(3):
                mm(ps[b][0:96, 0:512], wqu[:, c, h, :], cqT[:, c, cols], c == 0, c == 2, [], ['ps%d' % b])
            rope(qTb[k], 'qTb%d' % k, b, tc, 1, 96, t1s, t2s, cnt['rope'], stg)
            cnt['rope'] += 1

        def bk_item(h, tc):
            k = h % 2
            b = pj()
            cols = slice(tc * 512, (tc + 1) * 512)
            for c in range(2):
                mm(ps[b][0:96, 0:512], kupk[:, c, h, :], ckvT[:, c, cols], c == 0, False, [], ['ps%d' % b])
            mm(ps[b][0:96, 0:512], ident[0:48, 0:96], krT[0:48, cols], False, True, [], ['ps%d' % b])
            cp('act', kTb[k][0:96, cols], ps[b][0:96, 0:512], ['ps%d' % b], ['kTb%d' % k])

        def bv_item(h, half):
            k = h % 2
            b = pj()
            for tl in range(8):
                t_ = half * 8 + tl
                for c in range(2):
                    mm(ps[b][:, tl * 64:(tl + 1) * 64], ckvT[:, c, t_ * 128:(t_ + 1) * 128], kupv[:, c, h, :],
                       c == 0, c == 1, [], ['ps%d' % b])
            cp('dve', vb[k][:, half * 8:(half + 1) * 8, 0:64],
               ps[b][:, 0:512].rearrange("p (a b) -> p a b", a=8), ['ps%d' % b], ['vb%d' % k])

        def projB_items(h):
            its = [(bq_item, (h, tc)) for tc in range(4)]
            its += [(bk_item, (h, tc)) for tc in range(4)]
            its += [(bv_item, (h, hf)) for hf in range(2)]
            return its

        ITS = [(qg, kb) for qg in range(4) for kb in range(4 * qg + 4)]

        def geom(qg, kb):
            jd = kb - 4 * qg
            q0 = (4 * qg + max(jd, 0)) * 128
            nq = (qg + 1) * 512 - q0
            return jd, q0, nq

        def b_qk(h, i):
            k = h % 2
            qg, kb = ITS[i]
            jd, q0, nq = geom(qg, kb)
            gi = h * 40 + i
            sb_ = STB[gi % 4]
            qn = ['qTb%dlo' % k, 'qTb%dhi' % k, 'kTb%d' % k]
            mm(ps[sb_][:, 0:nq], kTb[k][0:96, kb * 128:(kb + 1) * 128], qTb[k][0:96, q0:q0 + nq],
               True, jd < 0, qn, ['ps%d' % sb_])
            if jd >= 0:
                mm(ps[sb_][:, 0:128], ident, maskcur, False, True, [], ['ps%d' % sb_])

        def b_exp(h, i):
            qg, kb = ITS[i]
            jd, q0, nq = geom(qg, kb)
            gi = h * 40 + i
            sb_ = STB[gi % 4]
            pk = gi % 4
            pn = 'pTb%d' % pk
            act(pTb[pk][:, 0:nq], ps[sb_][:, 0:nq], AF.Exp, ['ps%d' % sb_], [pn], scale=SCALE_B)

        def b_pv(h, i):
            k = h % 2
            qg, kb = ITS[i]
            jd, q0, nq = geom(qg, kb)
            gi = h * 40 + i
            pk = gi % 4
            pn = 'pTb%d' % pk
            ob_ = OVB[(h * 4 + qg) % 2]
            on = 'ps%d' % ob_
            for ii in range(nq // 128):
                jq = max(jd, 0) + ii
                mm(ps[ob_][:, jq * 65:(jq + 1) * 65], pTb[pk][:, ii * 128:(ii + 1) * 128], vb[k][:, kb, 0:65],
                   kb == 0 and ii == 0, kb == 4 * qg + jq, [pn, 'vb%d' % k], [on])
            if kb == 4 * qg + 3:
                rk = qg % 2
                ov = ps[ob_][:, 0:260].rearrange("p (a b) -> p a b", a=4)
                recip(rl[rk][:], ov[:, :, 64], [on], ['rl%d' % rk])
                tt('dve', obuf[:, 4 * qg:4 * qg + 4, h * 64:(h + 1) * 64], ov[:, :, 0:64],
                   rl[rk][:].unsqueeze(2).broadcast_to([128, 4, 64]), ALU.mult, [on, 'rl%d' % rk], ['obuf'])

        for f, a in projB_items(0):
            f(*a)
        NI = len(ITS)
        for h in range(16):
            nxt = projB_items(h + 1) if h + 1 < 16 else []
            b_qk(h, 0)
            b_exp(h, 0)
            b_qk(h, 1)
            ni = 0
            for i in range(NI):
                if i + 2 < NI:
                    b_qk(h, i + 2)
                if i + 1 < NI:
                    b_exp(h, i + 1)
                b_pv(h, i)
                target = min(len(nxt), (len(nxt) * (i + 1) + 27) // 28)
                while ni < target:
                    f, a = nxt[ni]
                    f(*a)
                    ni += 1
        S.barrier()

        loc = Bump(LOC_OFF, LOC_END)
        wbz = loc("wbz", [128, 8, 1024], BF16)
        woutb = loc("woutb", [128, 8, 1024], BF16)
        hnb = [loc("hnbC%d" % i, [128, D], BF16) for i in range(2)]
        hTt = [loc("hTt%d" % i, [128, 8, 128], BF16) for i in range(2)]
        thz = [loc("thz%d" % i, [128, D], F32) for i in range(2)]
        ybt = [loc("ybt%d" % i, [128, D], BF16) for i in range(2)]
        yTt = [loc("yTt%d" % i, [128, 8, 128], BF16) for i in range(2)]
        junk = [loc("junkC%d" % i, [128, 512], BF16) for i in range(2)]
        tmpf = [loc("tmpfC%d" % i, [128, D], F32) for i in range(2)]
        ost = [loc("ost%d" % i, [128, D], F32) for i in range(2)]
        for c_ in range(8):
            dma('pool', wbz[:, c_, :], wbz_d[:, c_ * 1024:(c_ + 1) * 1024], 'wbz%d' % c_, (), ['wbz%d' % c_])
        for c_ in range(8):
            dma('pool', woutb[:, c_, :], woutb_d[:, c_ * 1024:(c_ + 1) * 1024], 'wout%d' % c_, (), ['wout%d' % c_])
        TA = [4, 5]
        TB = [6, 7]

        def p5_a1(t_):
            k2 = t_ % 2
            ts('dve', hnb[k2][:], h1[:, t_, :], rstdb[:, t_:t_ + 1], None, ALU.mult, None, [], ['hnb%d' % k2])

        def p5_a2(t_):
            k2 = t_ % 2
            ba = TA[k2]
            for c in range(8):
                tr(psb[ba][:, c * 128:(c + 1) * 128], hnb[k2][:, c * 128:(c + 1) * 128], ['hnb%d' % k2],
                   ['ps%d' % ba])
            tt('dve', hTt[k2][:], psb[ba][:, 0:1024].rearrange("p (c t) -> p c t", c=8),
               gains[:, 16:24].unsqueeze(2).broadcast_to([128, 8, 128]), ALU.mult, ['ps%d' % ba], ['hTt%d' % k2])

        def p5_b(t_):
            k2 = t_ % 2
            for hf in range(2):
                b = [[0, 1], [2, 3]][k2][hf]
                hc = slice(hf * 512, (hf + 1) * 512)
                for c in range(8):
                    mm(ps[b][:, 0:512], hTt[k2][:, c, :], wbz[:, c, hc], c == 0, c == 7,
                       ['hTt%d' % k2, 'wbz%d' % c], ['ps%d' % b])
                act(thz[k2][:, hc], ps[b][:, 0:512], AF.Tanh, ['ps%d' % b], ['thz%d' % k2], scale=0.5)
                stt('dve', thz[k2][:, hc], thz[k2][:, hc], 1.0, ps[b][:, 0:512], ALU.add, ALU.mult,
                    ['thz%d' % k2, 'ps%d' % b], ['thz%d' % k2])
            stt('dve', obuf[:, t_, :], thz[k2][:], 0.5, obuf[:, t_, :], ALU.mult, ALU.mult, ['thz%d' % k2],
                ['ob%d' % t_])

        def p5_c(t_):
            k2 = t_ % 2
            bb = TB[k2]
            for c in range(8):
                tr(psb[bb][:, c * 128:(c + 1) * 128], obuf[:, t_, c * 128:(c + 1) * 128], ['ob%d' % t_],
                   ['ps%d' % bb])
            cp('act', yTt[k2][:], psb[bb][:, 0:1024].rearrange("p (c t) -> p c t", c=8), ['ps%d' % bb],
               ['yTt%d' % k2])

        def p5_d(t_):
            k2 = t_ % 2
            for hf in range(2):
                b = [[0, 1], [2, 3]][k2][hf]
                for c in range(8):
                    mm(ps[b][:, 0:512], yTt[k2][:, c, :], woutb[:, c, hf * 512:(hf + 1) * 512], c == 0, c == 7,
                       ['wout%d' % c, 'yTt%d' % k2], ['ps%d' % b])

        def p5_e(t_):
            k2 = t_ % 2
            b0, b1 = [[0, 1], [2, 3]][k2]
            c0_ = 16 + 4 * k2
            act(junk[0][:], ps[b0][:, 0:512], AF.Square, ['ps%d' % b0], ['junk0', 'pq0'], accum=stat[:, c0_:c0_ + 1])
            act(junk[1][:], ps[b1][:, 0:512], AF.Square, ['ps%d' % b1], ['junk1', 'pq1'],
                accum=stat[:, c0_ + 1:c0_ + 2])
            tt('dve', stat[:, c0_ + 2:c0_ + 3], stat[:, c0_:c0_ + 1], stat[:, c0_ + 1:c0_ + 2], ALU.add,
               ['pq0', 'pq1'], ['pnin'])
            rsqrt_col(stat[:, c0_ + 3:c0_ + 4], stat[:, c0_ + 2:c0_ + 3], D, 'pn')
            for hf, b in ((0, b0), (1, b1)):
                stt('dve', tmpf[k2][:, hf * 512:(hf + 1) * 512], ps[b][:, 0:512], stat[:, c0_ + 3:c0_ + 4],
                    gpost[:, 1, hf * 512:(hf + 1) * 512], ALU.mult, ALU.mult, ['ps%d' % b, 'pnrs'],
                    ['tmpf%d' % k2])
            tt('pool', ost[k2][:], h1[:, t_, :], tmpf[k2][:], ALU.add, ['tmpf%d' % k2], ['ost%d' % k2])
            dma('sp', out[s, t_ * 128:(t_ + 1) * 128, :], ost[k2][:], 'ost%d' % k2, ['ost%d' % k2],
                ['out%d_%d' % (s, t_)])

        tmps = tab_tmps(loc)

        def p5_t(t_):
            if s + 1 < NSEQ and t_ % 4 == 1:
                tab_prep(s + 1, 29, t_ // 4, tmps)

        pipeline([p5_a1, p5_a2, p5_b], NT, newest_first=True)
        pipeline([p5_c, p5_d, p5_e, p5_t], NT, newest_first=True)
        if s + 1 < NSEQ:
            tab_sin()
        S.barrier()

    S.emit()
    return nc


def _chunked(w):
    K, N = w.shape
    return np.ascontiguousarray(w.reshape(K // 128, 128, N).transpose(1, 0, 2))


def _prep_weights(inp):
    f32 = np.float32
    perm = np.concatenate([np.arange(0, 16), np.arange(32, 48), np.arange(16, 32), np.arange(48, 64),
                           np.arange(64, 128)])
    w_in = np.asarray(inp['a_w_in'][0], f32)
    wqkv = np.empty((24, 128, 8, 384), f32)
    for hs in range(8):
        for g in range(3):
            u = hs * 3 + g
            for j in range(3):
                c0 = g * 3072 + j * 1024 + hs * 128
                blk = w_in[:, c0:c0 + 128]
                if j < 2:
                    blk = blk[:, perm]
                wqkv[u, :, :, j * 128:(j + 1) * 128] = _chunked(blk)
    wz = np.empty((8, 128, 8, 128), f32)
    for hs in range(8):
        wz[hs] = _chunked(w_in[:, 9216 + hs * 128:9216 + (hs + 1) * 128])
    wouta = _chunked(np.asarray(inp['a_w_out'][0], f32))
    kvdw = np.asarray(inp['kv_w_down'], f32)
    kvd = np.zeros((1024, 304), f32)
    kvd[:, 0:256] = kvdw[:, 0:256]
    kvd[:, 256:272] = kvdw[:, 256:272]
    kvd[:, 288:304] = kvdw[:, 272:288]
    kvd = _chunked(kvd)
    kup = np.asarray(inp['kv_w_up'], f32).reshape(256, 16, 128)
    kupk = np.zeros((256, 16, 96), f32)
    kupk[:, :, 16:32] = kup[:, :, 0:16]
    kupk[:, :, 48:96] = kup[:, :, 16:64]
    kupk = _chunked(kupk.reshape(256, 16 * 96))
    kupv = _chunked(np.ascontiguousarray(kup[:, :, 64:128]).reshape(256, 16 * 64))
    bw_in = np.asarray(inp['b_w_in'][0], f32)
    wbq = _chunked(np.ascontiguousarray(bw_in[:, 0:384]))
    wbz = _chunked(np.ascontiguousarray(bw_in[:, 384:1408]))
    qu = np.asarray(inp['b_w_q_up'][0], f32).reshape(384, 16, 96)
    wqu = np.empty((384, 16, 96), f32)
    wqu[:, :, 0:16] = qu[:, :, 64:80]
    wqu[:, :, 16:32] = qu[:, :, 0:16]
    wqu[:, :, 32:48] = qu[:, :, 80:96]
    wqu[:, :, 48:96] = qu[:, :, 16:64]
    wqu = _chunked(wqu.reshape(384, 16 * 96))
    woutb = _chunked(np.asarray(inp['b_w_out'][0], f32))

    gains = np.zeros((128, 40), f32)
    gains[:, 0:8] = np.asarray(inp['a_pre_norm'][0], f32).reshape(8, 128).T
    gains[:, 8:16] = np.asarray(inp['kv_norm'], f32).reshape(8, 128).T
    gains[:, 16:24] = np.asarray(inp['b_pre_norm'][0], f32).reshape(8, 128).T
    gains[:, 24:26] = np.asarray(inp['kv_latent_norm'], f32).reshape(2, 128).T
    gains[:, 26:29] = np.asarray(inp['b_q_norm'][0], f32).reshape(3, 128).T
    i16 = np.arange(16, dtype=np.float64)
    for col, theta in ((29, 500000.0), (32, 10000.0)):
        invf = 1.0 / (theta ** (i16 * (2.0 / 32)))
        fu = np.zeros(64)
        fu[0:16] = invf / (2 * np.pi)
        fu[32:48] = invf / (2 * np.pi)
        gains[0:64, col] = fu.astype(f32)
        gains[64:128, col] = fu.astype(f32)
    gains[:, 30] = 0.25
    gains[0:16, 31] = 0.5
    gains[64:80, 31] = 0.5
    gpost = np.stack([np.asarray(inp['a_post_norm'][0], f32), np.asarray(inp['b_post_norm'][0], f32)])
    kk = np.arange(128)[:, None]
    qq = np.arange(128)[None, :]
    cst = np.zeros((128, 768), f32)
    cst[:, 512:640] = (kk >= qq)
    cst[:, 640:768] = (kk <= qq)
    cst[:, 0:128] = np.eye(128, dtype=f32)
    cst[:, 128:256] = np.where(kk >= qq, 0.0, -30000.0)
    cst[:, 256:384] = np.where(kk <= qq, 0.0, -30000.0)
    cst[:, 384:512] = 1.0
    return dict(
        wqkv=np.ascontiguousarray(wqkv.reshape(24, 128, 8 * 384)),
        wz=np.ascontiguousarray(wz.reshape(8, 128, 8 * 128)),
        wouta=wouta.reshape(128, -1), kvd=kvd.reshape(128, -1), kupk=kupk.reshape(128, -1),
        kupv=kupv.reshape(128, -1), wbq=wbq.reshape(128, -1), wbz=wbz.reshape(128, -1),
        wqu=wqu.reshape(128, -1), woutb=woutb.reshape(128, -1), gains=gains, gpost=gpost, cst=cst)


def kernel(**inputs):
    x = np.asarray(inputs['x'], np.float32)
    positions = np.asarray(inputs['positions'], np.int32)
    B = x.shape[0]
    nseq = B // NCORES
    w = _prep_weights(inputs)
    w = {k: np.ascontiguousarray(v, dtype=np.float32) for k, v in w.items()}
    nc = build_program(nseq)
    in_maps = []
    for c in range(NCORES):
        m = dict(w)
        m['x'] = np.ascontiguousarray(x[c * nseq:(c + 1) * nseq])
        m['pos'] = np.ascontiguousarray(positions[c * nseq:(c + 1) * nseq])
        in_maps.append(m)
    res = run_bass_kernel_spmd(nc, in_maps, core_ids=list(range(NCORES)))
    return np.concatenate([np.asarray(r['out'], np.float32) for r in res.results], axis=0)
```

```python
import contextlib
import numpy as np
import concourse.bass as bass
import concourse.mybir as mybir
from concourse.bass_utils import run_bass_kernel_spmd

F32 = mybir.dt.float32
BF16 = mybir.dt.bfloat16
I32 = mybir.dt.int32
AF = mybir.ActivationFunctionType
ALU = mybir.AluOpType

NCORES = 8
T = 2048
D = 1024
NT = 16
EPS = 1e-6
DIL = (1, 4, 16)
SCALE_A = 128 ** -0.5
SCALE_B = 96 ** -0.5
TWO_PI = float(2 * np.pi)
ENGS = ['pe', 'act', 'dve', 'pool', 'sp']


class Op:
    __slots__ = ('eng', 'fn', 'deps', 'needs_inc', 'semval', 'chan', 'chanval')

    def __init__(self, eng, fn, chan=None):
        self.eng = eng
        self.fn = fn
        self.deps = []
        self.needs_inc = False
        self.semval = 0
        self.chan = chan
        self.chanval = 0


class Sched:
    def __init__(self, nc):
        self.nc = nc
        self.ops = {e: [] for e in ENGS}
        self.last_w = {}
        self.readers = {}
        self.chan_count = {}
        self.chan_last = {}

    def add(self, eng, fn, reads=(), writes=(), chan=None):
        op = Op(eng, fn, chan)
        deps = {}
        for r in reads:
            w = self.last_w.get(r)
            if w is not None:
                deps[id(w)] = (w, True)
        for wr in writes:
            w = self.last_w.get(wr)
            if w is not None and id(w) not in deps:
                deps[id(w)] = (w, False)
            for rd in self.readers.get(wr, {}).values():
                if id(rd) not in deps:
                    deps[id(rd)] = (rd, False)
        for d, raw in deps.values():
            if d is op:
                continue
            if d.chan is None and chan is None and d.eng == eng:
                if eng == 'pe' or not raw:
                    continue
            op.deps.append(d)
            if d.chan is None:
                d.needs_inc = True
        key = eng if chan is None else ('dma', chan)
        for r in reads:
            self.readers.setdefault(r, {})[key] = op
        for wr in writes:
            self.last_w[wr] = op
            self.readers[wr] = {}
        if chan is not None:
            self.chan_count[chan] = self.chan_count.get(chan, 0) + 16
            op.chanval = self.chan_count[chan]
            self.chan_last[chan] = op
        self.ops[eng].append(op)
        return op

    def barrier(self):
        lasts = [self.ops[e][-1] for e in ENGS if self.ops[e]]
        dlast = list(self.chan_last.values())
        newops = []
        for e in ENGS:
            op = Op(e, lambda eng: None)
            for d in lasts:
                if d.chan is not None:
                    continue
                if d.eng == e and e in ('pe', 'sp'):
                    continue
                op.deps.append(d)
                d.needs_inc = True
            for d in dlast:
                op.deps.append(d)
            newops.append(op)
        for op in newops:
            self.ops[op.eng].append(op)
        self.last_w.clear()
        self.readers.clear()
        self.chan_last.clear()

    def emit(self):
        nc = self.nc
        with contextlib.ExitStack() as st:
            esem = {e: st.enter_context(nc.semaphore('s_' + e)) for e in ENGS}
            csem = {c: st.enter_context(nc.semaphore('c_' + str(c))) for c in self.chan_count}
            for e in ENGS:
                cnt = 0
                for op in self.ops[e]:
                    if op.chan is None and op.needs_inc:
                        cnt += 1
                        op.semval = cnt

            def run(e, eng):
                waited = {}
                for op in self.ops[e]:
                    for d in op.deps:
                        if d.chan is not None:
                            sem, val, key = csem[d.chan], d.chanval, ('c', d.chan)
                        else:
                            sem, val, key = esem[d.eng], d.semval, ('e', d.eng)
                        if waited.get(key, 0) < val:
                            eng.wait_ge(sem, val)
                            waited[key] = val
                    ins = op.fn(eng)
                    if ins is None:
                        if op.needs_inc:
                            eng.sem_inc(esem[e], 1)
                        continue
                    if op.chan is not None:
                        ins.then_inc(csem[op.chan], 16)
                    elif op.needs_inc:
                        ins.then_inc(esem[e], 1)

            with nc.Block() as block:
                @block.tensor
                def _(eng):
                    run('pe', eng)

                @block.scalar
                def _(eng):
                    run('act', eng)

                @block.vector
                def _(eng):
                    run('dve', eng)

                @block.gpsimd
                def _(eng):
                    run('pool', eng)

                @block.sync
                def _(eng):
                    run('sp', eng)


def sl(start, count, step=1):
    return slice(start, start + (count - 1) * step + 1, step)


def build_program(NSEQ):
    nc = bass.Bass("TRN2", target_bir_lowering=False)
    S = Sched(nc)

    def dram(name, shape, dt, kind="ExternalInput"):
        return nc.dram_tensor(name, list(shape), dt, kind=kind).ap()

    x = dram("x", [NSEQ, T, D], F32)
    pos = dram("pos", [NSEQ, T], I32)
    wqkv_d = dram("wqkv", [24, 128, 8 * 384], F32)
    wz_d = dram("wz", [8, 128, 8 * 128], F32)
    wouta_d = dram("wouta", [128, 8 * 1024], F32)
    kvd_d = dram("kvd", [128, 8 * 304], F32)
    kupk_d = dram("kupk", [128, 2 * 16 * 96], F32)
    kupv_d = dram("kupv", [128, 2 * 16 * 64], F32)
    wbq_d = dram("wbq", [128, 8 * 384], F32)
    wbz_d = dram("wbz", [128, 8 * 1024], F32)
    wqu_d = dram("wqu", [128, 3 * 16 * 96], F32)
    woutb_d = dram("woutb", [128, 8 * 1024], F32)
    gains_d = dram("gains", [128, 40], F32)
    gpost_d = dram("gpost", [2, 1024], F32)
    cst_d = dram("cst", [128, 768], F32)
    out = dram("out", [NSEQ, T, D], F32, kind="ExternalOutput")

    BASE = 16512
    KB = 1024

    def sb(name, shape, dt, off):
        assert off % 32 == 0, (name, off)
        nbytes = int(np.prod(shape[1:])) * (2 if dt == BF16 else 4)
        assert off + nbytes <= 229344 - BASE, (name, off, nbytes)
        return nc.alloc_sbuf_tensor_at(name, list(shape), dt, offset=BASE + off)

    class Bump:
        def __init__(self, start, end):
            self.p = start
            self.end = end

        def __call__(self, name, shape, dt):
            nbytes = int(np.prod(shape[1:])) * (2 if dt == BF16 else 4)
            nbytes = (nbytes + 63) // 64 * 64
            t = sb(name, shape, dt, self.p)
            self.p += nbytes
            assert self.p <= self.end, (name, self.p, self.end)
            return t

    pb = Bump(0, 10 * KB)
    cstb = pb("cstb", [128, 768], BF16)
    gains = pb("gains", [128, 40], F32)
    gpost = pb("gpost", [128, 2, 1024], F32)
    stat = pb("stat", [128, 64], F32)
    rstdb = pb("rstdb", [128, 16], F32)
    ident = cstb[:, 0:128]
    maskA = cstb[:, 128:384]
    maskcur = cstb[:, 256:384]
    ones = cstb[:, 384:512]
    mask01 = cstb[:, 512:768]

    H1_OFF = 10 * KB
    YT_OFF = 74 * KB
    TAB_OFF = 106 * KB
    LOC_OFF = 122 * KB
    LOC_END = 229344 - BASE

    ps = [nc.alloc_psum_tensor("ps%d" % i, [128, 512], F32) for i in range(8)]
    psb = [p.bitcast(BF16) for p in ps]

    tabC = sb("tabC", [128, T], F32, TAB_OFF)
    tabS = sb("tabS", [128, T], F32, TAB_OFF + 8 * KB)

    def mm(out_, lhsT, rhs, start, stop, reads, writes):
        S.add('pe', lambda e: e.matmul(out_, lhsT, rhs, start=start, stop=stop), reads, writes)

    def tr(out_, in_, reads, writes):
        S.add('pe', lambda e: e.transpose(out_, in_, ident), reads, writes)

    def act(out_, in_, func, reads, writes, scale=1.0, bias=0.0, accum=None):
        S.add('act', lambda e: e.activation(out=out_, in_=in_, func=func, bias=bias, scale=scale,
                                            accum_out=accum), reads, writes)

    def tt(eng, out_, in0, in1, op, reads, writes):
        S.add(eng, lambda e: e.tensor_tensor(out=out_, in0=in0, in1=in1, op=op), reads, writes)

    def ts(eng, out_, in0, s1, s2, op0, op1, reads, writes):
        if s2 is None:
            S.add(eng, lambda e: e.tensor_scalar(out=out_, in0=in0, scalar1=s1, scalar2=None, op0=op0),
                  reads, writes)
        else:
            S.add(eng, lambda e: e.tensor_scalar(out=out_, in0=in0, scalar1=s1, scalar2=s2, op0=op0, op1=op1),
                  reads, writes)

    def stt(eng, out_, in0, scalar, in1, op0, op1, reads, writes):
        S.add(eng, lambda e: e.scalar_tensor_tensor(out=out_, in0=in0, scalar=scalar, in1=in1, op0=op0, op1=op1),
              reads, writes)

    def cp(eng, out_, in_, reads, writes):
        if eng == 'act':
            S.add('act', lambda e: e.copy(out=out_, in_=in_), reads, writes)
        else:
            S.add(eng, lambda e: e.tensor_copy(out=out_, in_=in_), reads, writes)

    def recip(out_, in_, reads, writes):
        S.add('dve', lambda e: e.reciprocal(out=out_, in_=in_), reads, writes)

    def memset(eng, out_, val, writes):
        S.add(eng, lambda e: e.memset(out_, val), (), writes)

    def dma(q, out_, in_, chan, reads, writes):
        S.add(q, lambda e: e.dma_start(out=out_, in_=in_), reads, writes, chan=chan)

    class Rot:
        def __init__(self, banks):
            self.banks = banks
            self.i = 0

        def __call__(self):
            b = self.banks[self.i % len(self.banks)]
            self.i += 1
            return b

    dma('pool', cstb[:], cst_d, 'cst', (), ['cst'])
    dma('sp', gains[:], gains_d, 'gains', (), ['gains'])
    dma('sp', gpost[:, 0, :], gpost_d[0, :].partition_broadcast(128), 'gpost0', (), ['gpost0'])
    dma('sp', gpost[:, 1, :], gpost_d[1, :].partition_broadcast(128), 'gpost1', (), ['gpost1'])
    S.barrier()

    def rsqrt_col(col_out, col_in, n, tag):
        act(col_out, col_in, AF.Sqrt, [tag + 'in'], [tag + 'sq', tag + 'rs'], scale=1.0 / n, bias=EPS)
        recip(col_out, col_out, [tag + 'sq'], [tag + 'rs'])

    def tab_tmps(loc):
        return (loc("posi", [128, 512], I32), loc("posf", [128, 512], F32), loc("tki", [128, 512], I32),
                loc("tkf", [128, 512], F32))

    def tab_prep(s, fcol, ch, tmps):
        posi, posf, ki, kf = tmps
        cols = slice(ch * 512, (ch + 1) * 512)
        dma('sp', posi[:], pos[s, cols].partition_broadcast(128), 'posi', (), ['posi'])
        cp('dve', posf[:], posi[:], ['posi'], ['posf'])
        for tab, tn, phcol in ((tabC, 'tabC', 30), (tabS, 'tabS', 31)):
            u = tab[:, cols]
            ts('dve', u, posf[:], gains[:, fcol:fcol + 1], gains[:, phcol:phcol + 1],
               ALU.mult, ALU.add, ['posf'], [tn])
            cp('dve', ki[:], u, [tn], ['tki'])
            cp('dve', kf[:], ki[:], ['tki'], ['tkf'])
            tt('dve', u, u, kf[:], ALU.subtract, [tn, 'tkf'], [tn])
            ts('dve', kf[:], u, 0.5, None, ALU.is_gt, None, [tn], ['tkf'])
            tt('dve', u, u, kf[:], ALU.subtract, [tn, 'tkf'], [tn])

    def tab_sin():
        for tab, tn in ((tabC, 'tabC'), (tabS, 'tabS')):
            act(tab[:, :], tab[:, :], AF.Sin, [tn], [tn], scale=TWO_PI)

    ropetmp = {}

    def rope(dst, dname, bank, tc, d, rows, t1s, t2s, k, stg):
        stgA, stgB = stg
        p = ps[bank]
        nl = 512 // d
        ns = len(t1s)
        t1 = t1s[k % ns]
        t2 = t2s[k % ns]
        ka, kb_ = k % len(stgA), k % len(stgB)
        sa, sb2 = stgA[ka], stgB[kb_]
        cols = slice(tc * 512, (tc + 1) * 512)

        def dv(p0, p1):
            if d == 1:
                return dst[p0:p1, cols]
            return dst[p0:p1, :].rearrange("p (r l) -> p r l", r=d)[:, :, tc * nl:(tc + 1) * nl]

        def sv(ap):
            if d == 1:
                return ap
            return ap.rearrange("p (l r) -> p r l", r=d)

        bn = 'ps%d' % bank
        n1, n2 = 't1_%d' % (k % ns), 't2_%d' % (k % ns)
        san, sbn = 'stgA%d' % ka, 'stgB%d' % kb_
        cp('act', sa[0:64, :], p[0:64, 0:512], [bn], [san])
        if rows > 64:
            cp('act', dv(64, rows), sv(p[64:rows, 0:512]), [bn], [dname + 'hi'])
        dma('sp', sb2[0:32, :], sa[32:64, :], sbn, [san], [sbn + 'a'])
        dma('sp', sb2[32:64, :], sa[0:32, :], sbn, [san], [sbn + 'b'])
        tt('dve', t1[0:64, :], sa[0:64, :], tabC[0:64, cols], ALU.mult, [san], [n1])
        tt('dve', t2[0:64, :], sb2[0:64, :], tabS[0:64, cols], ALU.mult, [sbn + 'a', sbn + 'b'], [n2])
        tt('dve', dv(0, 64), sv(t1[0:64, :]), sv(t2[0:64, :]), ALU.add, [n1, n2], [dname + 'lo'])

    def pipeline(stages, n, newest_first=False):
        ns = len(stages)
        for i in range(n + ns - 1):
            ks = range(ns) if newest_first else reversed(range(ns))
            for k in ks:
                t_ = i - k
                if 0 <= t_ < n:
                    stages[k](t_)

    def rope_pair(dq, dk, qname, kname, bq, bk, tc, d, t1s, t2s, k, stg):
        stgA, stgB = stg
        nl = 512 // d
        ns = len(t1s)
        t1 = t1s[k % ns]
        t2 = t2s[k % ns]
        ka, kb_ = k % len(stgA), k % len(stgB)
        sa, sb2 = stgA[ka], stgB[kb_]
        cols = slice(tc * 512, (tc + 1) * 512)

        def dv(dst, p0, p1):
            if d == 1:
                return dst[p0:p1, cols]
            return dst[p0:p1, :].rearrange("p (r l) -> p r l", r=d)[:, :, tc * nl:(tc + 1) * nl]

        def sv(ap):
            if d == 1:
                return ap
            return ap.rearrange("p (l r) -> p r l", r=d)

        n1, n2 = 't1_%d' % (k % ns), 't2_%d' % (k % ns)
        san, sbn = 'stgA%d' % ka, 'stgB%d' % kb_
        bqn, bkn = 'ps%d' % bq, 'ps%d' % bk
        cp('act', sa[0:64, :], ps[bq][0:64, 0:512], [bqn], [san + 'q'])
        cp('act', sa[64:128, :], ps[bk][0:64, 0:512], [bkn], [san + 'k'])
        cp('act', dv(dq, 64, 128), sv(ps[bq][64:128, 0:512]), [bqn], [qname + 'hi'])
        cp('act', dv(dk, 64, 128), sv(ps[bk][64:128, 0:512]), [bkn], [kname + 'hi'])
        dma('sp', sb2[0:32, :], sa[32:64, :], sbn, [san + 'q'], [sbn + 'a'])
        dma('sp', sb2[32:64, :], sa[0:32, :], sbn, [san + 'q'], [sbn + 'b'])
        dma('sp', sb2[64:96, :], sa[96:128, :], sbn, [san + 'k'], [sbn + 'c'])
        dma('sp', sb2[96:128, :], sa[64:96, :], sbn, [san + 'k'], [sbn + 'd'])
        tt('dve', t1[:, :], sa[:, :], tabC[:, cols], ALU.mult, [san + 'q', san + 'k'], [n1])
        tt('dve', t2[:, :], sb2[:, :], tabS[:, cols], ALU.mult, [sbn + x for x in 'abcd'], [n2])
        tt('dve', dv(dq, 0, 64), sv(t1[0:64, :]), sv(t2[0:64, :]), ALU.add, [n1, n2], [qname + 'lo'])
        tt('dve', dv(dk, 0, 64), sv(t1[64:128, :]), sv(t2[64:128, :]), ALU.add, [n1, n2], [kname + 'lo'])

    for s in range(NSEQ):
        hnT = sb("hnT", [128, 8, T], BF16, H1_OFF)
        loc = Bump(LOC_OFF, LOC_END)
        if s == 0:
            tmps = tab_tmps(loc)
            for ch in range(4):
                tab_prep(s, 29, ch, tmps)
            tab_sin()
        xt = [loc("xt%d" % i, [128, D], F32) for i in range(3)]
        hnb = [loc("hnb%d" % i, [128, D], BF16) for i in range(2)]
        def p1_a(t_):
            k3, k2 = t_ % 3, t_ % 2
            dma('sp', xt[k3][:], x[s, t_ * 128:(t_ + 1) * 128, :], 'xt%d' % k3, (), ['xt%d' % k3])
            act(hnb[k2][:], xt[k3][:], AF.Square, ['xt%d' % k3], ['hnb%d' % k2, 'ssqin'], accum=stat[:, 0:1])
            rsqrt_col(stat[:, 8 + k2:9 + k2], stat[:, 0:1], D, 'ssq')
            act(hnb[k2][:], xt[k3][:], AF.Copy, ['xt%d' % k3, 'ssqrs'], ['hnb%d' % k2],
                scale=stat[:, 8 + k2:9 + k2])

        def p1_b(t_):
            k2 = t_ % 2
            b = k2
            for c in range(8):
                tr(psb[b][:, c * 128:(c + 1) * 128], hnb[k2][:, c * 128:(c + 1) * 128],
                   ['hnb%d' % k2], ['ps%d' % b])
            tt('dve', hnT[:, :, t_ * 128:(t_ + 1) * 128],
               psb[b][:, 0:1024].rearrange("p (c t) -> p c t", c=8),
               gains[:, 0:8].unsqueeze(2).broadcast_to([128, 8, 128]), ALU.mult,
               ['ps%d' % b], ['hnT'])

        pipeline([p1_a, p1_b], NT, newest_first=True)
        S.barrier()

        yT = sb("yT", [128, 8, T], BF16, YT_OFF)
        hb = Bump(H1_OFF + 32 * KB, H1_OFF + 64 * KB)
        qT = [hb("qT%d" % i, [128, T], BF16) for i in range(2)]
        kT = [hb("kT%d" % i, [128, T], BF16) for i in range(2)]
        vv = [hb("v%d" % i, [128, 16, 128], BF16) for i in range(2)]
        gz = [hb("gz%d" % i, [128, T], BF16) for i in range(2)]
        loc = Bump(LOC_OFF, LOC_END)
        OL = [loc("OL%d" % i, [128, 2, T], F32) for i in range(2)]
        wq = [loc("wqkv%d" % i, [128, 8, 384], BF16) for i in range(2)]
        wzs = [loc("wz%d" % i, [128, 8, 128], BF16) for i in range(2)]
        th = [loc("th%d" % i, [128, 512], F32) for i in range(2)]
        t1s = [loc("t1_%d" % i, [128, 512], F32) for i in range(2)]
        t2s = [loc("t2_%d" % i, [128, 512], F32) for i in range(2)]
        stg = ([loc("stgA%d" % i, [128, 512], F32) for i in range(3)],
               [loc("stgB%d" % i, [128, 512], F32) for i in range(3)])
        ostg = [loc("ostg%d" % i, [128, 2, 128], F32) for i in range(4)]
        pT = [loc("pT%d" % i, [128, 256], BF16) for i in range(4)]
        pS = [loc("pS%d" % i, [128, 128], BF16) for i in range(4)]
        pj = Rot([0, 1, 2, 7])
        STB = [3, 4]
        OVB = [5, 6]
        cnt = {'rope': 0, 'th': 0}

        def load_unit(u):
            dma('pool', wq[u % 2][:].rearrange("p c n -> p (c n)"), wqkv_d[u], 'wqkv%d' % (u % 2),
                (), ['wqkv%d' % (u % 2)])

        def load_wz(hs):
            dma('pool', wzs[hs % 2][:].rearrange("p c n -> p (c n)"), wz_d[hs], 'wz%d' % (hs % 2),
                (), ['wz%d' % (hs % 2)])

        def gate_item(hs, tc):
            hk = hs % 2
            b = pj()
            cols = slice(tc * 512, (tc + 1) * 512)
            for c in range(8):
                mm(ps[b][:, 0:512], wzs[hk][:, c, :], hnT[:, c, cols], c == 0, c == 7,
                   ['wz%d' % hk], ['ps%d' % b])
            tk = cnt['th'] % 2
            cnt['th'] += 1
            act(th[tk][:], ps[b][:, 0:512], AF.Tanh, ['ps%d' % b], ['th%d' % tk], scale=0.5)
            stt('dve', gz[hk][:, cols], th[tk][:], 1.0, ps[b][:, 0:512], ALU.add, ALU.mult,
                ['th%d' % tk, 'ps%d' % b], ['gz%d' % hk])

        def qk_item(u, tc):
            k = u % 2
            d = DIL[u % 3]
            bq, bk = pj(), pj()
            cols = slice(tc * 512, (tc + 1) * 512)
            for j, b in ((0, bq), (1, bk)):
                for c in range(8):
                    mm(ps[b][:, 0:512], wq[k][:, c, j * 128:(j + 1) * 128], hnT[:, c, cols],
                       c == 0, c == 7, ['wqkv%d' % k], ['ps%d' % b])
            rope_pair(qT[k], kT[k], 'qT%d' % k, 'kT%d' % k, bq, bk, tc, d, t1s, t2s, cnt['rope'], stg)
            cnt['rope'] += 1

        def v_item(u, jb):
            k = u % 2
            d = DIL[u % 3]
            nb = (T // d) // 128
            b = pj()
            for jj in range(4):
                j = jb + jj
                r, n = j // nb, j % nb
                toks = sl(n * 128 * d + r, 128, d)
                for c in range(8):
                    mm(ps[b][:, jj * 128:(jj + 1) * 128], hnT[:, c, toks], wq[k][:, c, 256:384],
                       c == 0, c == 7, ['wqkv%d' % k], ['ps%d' % b])
            cp('act', vv[k][:, jb:jb + 4, :], ps[b][:, 0:512].rearrange("p (a b) -> p a b", a=4),
               ['ps%d' % b], ['v%d' % k])

        def proj_items(u):
            its = []
            for tc in range(4):
                its.append((qk_item, (u, tc)))
                its.append((v_item, (u, tc * 4)))
            return its

        def att_qk(u, j):
            k = u % 2
            nb = (T // DIL[u % 3]) // 128
            n = j % nb
            gi = u * 16 + j
            sb_ = STB[gi % 2]
            sn = 'ps%d' % sb_
            qn = ['qT%dlo' % k, 'qT%dhi' % k, 'kT%dlo' % k, 'kT%dhi' % k]
            qcols = slice(j * 128, (j + 1) * 128)
            c0 = 0 if n > 0 else 128
            if n > 0:
                mm(ps[sb_][:, 0:128], kT[k][:, (j - 1) * 128:j * 128], qT[k][:, qcols], True, False, qn, [sn])
            mm(ps[sb_][:, 128:256], kT[k][:, qcols], qT[k][:, qcols], n == 0, False, qn, [sn])
            mm(ps[sb_][:, c0:256], ident, maskA[:, c0:256], False, True, [], [sn])

        def att_exp(u, j):
            nb = (T // DIL[u % 3]) // 128
            n = j % nb
            gi = u * 16 + j
            sb_ = STB[gi % 2]
            sn = 'ps%d' % sb_
            c0 = 0 if n > 0 else 128
            pk = gi % 4
            pn = 'pT%d' % pk
            act(pT[pk][:, c0:256], ps[sb_][:, c0:256], AF.Exp, [sn], [pn], scale=SCALE_A)

        def att_pv(u, j):
            k = u % 2
            hs, g = divmod(u, 3)
            hk = hs % 2
            d = DIL[g]
            nb = (T // d) // 128
            r, n = j // nb, j % nb
            gi = u * 16 + j
            pk = gi % 4
            pn = 'pT%d' % pk
            ob_ = OVB[gi % 2]
            on = 'ps%d' % ob_
            vn = 'v%d' % k
            if n > 0:
                mm(ps[ob_][:, 0:128], vv[k][:, j - 1, :], pT[pk][:, 0:128], True, False, [pn, vn], [on])
            mm(ps[ob_][:, 0:128], vv[k][:, j, :], pT[pk][:, 128:256], n == 0, True, [pn, vn], [on])
            if n > 0:
                mm(ps[ob_][:, 128:256], ones, pT[pk][:, 0:128], True, False, [pn], [on])
            mm(ps[ob_][:, 128:256], ones, pT[pk][:, 128:256], n == 0, True, [pn], [on])
            toks = sl(n * 128 * d + r, 128, d)
            OLv = OL[hk][:, :, toks]
            src = ps[ob_][:, 0:256].rearrange("p (a b) -> p a b", a=2)
            if g == 0:
                quarters = [j // 4]
            elif g == 1:
                quarters = [n]
            else:
                quarters = [0, 1, 2, 3]
            oln = ['OL%d_%d' % (hk, q_) for q_ in quarters]
            if g == 0:
                cp('act', OLv, src, [on], oln)
            else:
                ok_ = gi % 4
                cp('act', ostg[ok_][:], src, [on], ['ostg%d' % ok_])
                tt('dve', OLv, ostg[ok_][:], OLv, ALU.add, ['ostg%d' % ok_] + oln, oln)

        def finalize_q(hs, q_):
            hk = hs % 2
            cols = slice(q_ * 512, (q_ + 1) * 512)
            on_ = ['OL%d_%d' % (hk, q_)]
            recip(OL[hk][:, 1, cols], OL[hk][:, 1, cols], on_, on_)
            tt('dve', OL[hk][:, 0, cols], OL[hk][:, 0, cols], OL[hk][:, 1, cols], ALU.mult, on_, on_)
            stt('dve', yT[:, hs, cols], OL[hk][:, 0, cols], 0.5, gz[hk][:, cols], ALU.mult, ALU.mult,
                on_ + ['gz%d' % hk], ['yT'])

        load_wz(0)
        load_unit(0)
        load_unit(1)
        for tc in range(4):
            gate_item(0, tc)
        for f, a in proj_items(0):
            f(*a)
        for u in range(24):
            hs, g = divmod(u, 3)
            nxt = []
            if u + 1 < 24:
                if (u + 1) % 3 == 0:
                    nxt += [(gate_item, (hs + 1, tc)) for tc in range(4)]
                nxt += proj_items(u + 1)
            if u + 2 < 24:
                load_unit(u + 2)
            if g == 0 and hs + 1 < 8:
                load_wz(hs + 1)
            att_qk(u, 0)
            att_exp(u, 0)
            att_qk(u, 1)
            ni = 0
            for j in range(16):
                if j + 2 < 16:
                    att_qk(u, j + 2)
                if j + 1 < 16:
                    att_exp(u, j + 1)
                att_pv(u, j)
                target = min(len(nxt), (len(nxt) * (j + 1) + 9) // 10)
                while ni < target:
                    f, a = nxt[ni]
                    f(*a)
                    ni += 1
                if g == 0 and hs > 0 and j in (3, 6, 9, 12):
                    finalize_q(hs - 1, (j - 3) // 3)
        for q_ in range(4):
            finalize_q(7, q_)
        S.barrier()

        h1 = sb("h1", [128, NT, D], F32, H1_OFF)
        loc = Bump(LOC_OFF, LOC_END)
        wouta = loc("wouta", [128, 8, 1024], BF16)
        junk = [loc("junk%d" % i, [128, 512], BF16) for i in range(2)]
        tmpf = [loc("tmpf%d" % i, [128, D], F32) for i in range(2)]
        for c_ in range(8):
            dma('pool', wouta[:, c_, :], wouta_d[:, c_ * 1024:(c_ + 1) * 1024], 'wout%d' % c_, (), ['wout%d' % c_])
        YB = [[0, 1], [2, 3]]

        def pn_mm(t_, yTt, wout_):
            b0, b1 = YB[t_ % 2]
            for hf, b in ((0, b0), (1, b1)):
                for c in range(8):
                    mm(ps[b][:, 0:512], yTt(c), wout_[:, c, hf * 512:(hf + 1) * 512], c == 0, c == 7,
                       ['wout%d' % c, 'yTt%d' % (t_ % 2)], ['ps%d' % b])

        def pn_fin(t_, gp, resid_tile, dst_tile, dst_name, resid_reads):
            k2 = t_ % 2
            b0, b1 = YB[k2]
            c0_ = 16 + 4 * k2
            act(junk[0][:], ps[b0][:, 0:512], AF.Square, ['ps%d' % b0], ['junk0', 'pq0'], accum=stat[:, c0_:c0_ + 1])
            act(junk[1][:], ps[b1][:, 0:512], AF.Square, ['ps%d' % b1], ['junk1', 'pq1'],
                accum=stat[:, c0_ + 1:c0_ + 2])
            tt('dve', stat[:, c0_ + 2:c0_ + 3], stat[:, c0_:c0_ + 1], stat[:, c0_ + 1:c0_ + 2], ALU.add,
               ['pq0', 'pq1'], ['pnin'])
            rsqrt_col(stat[:, c0_ + 3:c0_ + 4], stat[:, c0_ + 2:c0_ + 3], D, 'pn')
            for hf, b in ((0, b0), (1, b1)):
                stt('dve', tmpf[k2][:, hf * 512:(hf + 1) * 512], ps[b][:, 0:512], stat[:, c0_ + 3:c0_ + 4],
                    gp[:, hf * 512:(hf + 1) * 512], ALU.mult, ALU.mult, ['ps%d' % b, 'pnrs'], ['tmpf%d' % k2])
            tt('pool', dst_tile, resid_tile, tmpf[k2][:], ALU.add, ['tmpf%d' % k2] + resid_reads, [dst_name])

        def p3a_a(t_):
            dma('sp', h1[:, t_, :], x[s, t_ * 128:(t_ + 1) * 128, :], 'h1_%d' % t_, (), ['h1_%d' % t_])
            pn_mm(t_, lambda c: yT[:, c, t_ * 128:(t_ + 1) * 128], wouta)

        def p3a_b(t_):
            pn_fin(t_, gpost[:, 0, :], h1[:, t_, :], h1[:, t_, :], 'h1_%d' % t_, ['h1_%d' % t_])

        tmps = tab_tmps(loc)

        def p3a_t(t_):
            if t_ % 4 == 1:
                tab_prep(s, 32, t_ // 4, tmps)

        pipeline([p3a_a, p3a_b, p3a_t], NT)
        tab_sin()
        S.barrier()

        loc = Bump(LOC_OFF, LOC_END)
        cqT = loc("cqT", [128, 3, T], BF16)
        ckvT = loc("ckvT", [128, 2, T], BF16)
        krT = loc("krT", [64, T], BF16)
        wqu = loc("wqu", [128, 3, 16, 96], BF16)
        kupk = loc("kupk", [128, 2, 16, 96], BF16)
        kupv = loc("kupv", [128, 2, 16, 64], BF16)
        p4_start = loc.p
        yb = Bump(YT_OFF, YT_OFF + 32 * KB)
        hnb = [yb("hnbB%d" % i, [128, D], BF16) for i in range(2)]
        hTb = [loc("hTb%d" % i, [128, 8, 512], BF16) for i in range(2)]
        hTkv = [loc("hTkv%d" % i, [128, 8, 512], BF16) for i in range(2)]
        kvd = yb("kvd", [128, 8, 304], BF16)
        wbq = yb("wbq", [128, 8, 384], BF16)
        sq = yb("sq", [128, 3, 512], BF16)
        rsf = yb("rsf", [128, 512], F32)
        t1s = [yb("t1b%d" % i, [64, 512], F32) for i in range(2)]
        t2s = [yb("t2b%d" % i, [64, 512], F32) for i in range(2)]
        dma('pool', wqu[:].rearrange("p a b c -> p (a b c)"), wqu_d, 'wqu', (), ['wqu'])
        dma('pool', kupk[:].rearrange("p a b c -> p (a b c)"), kupk_d, 'kupk', (), ['kupk'])
        dma('pool', kupv[:].rearrange("p a b c -> p (a b c)"), kupv_d, 'kupv', (), ['kupv'])
        dma('pool', kvd[:].rearrange("p c n -> p (c n)"), kvd_d, 'kvd', (), ['kvd'])
        dma('pool', wbq[:].rearrange("p c n -> p (c n)"), wbq_d, 'wbq', (), ['wbq'])
        for k_ in range(2):
            memset('pool', t2s[k_][:], 0.0, ['t2_%da' % k_, 't2_%db' % k_])
        def p3b_ta(t_):
            k2 = t_ % 2
            act(hnb[k2][:], h1[:, t_, :], AF.Square, [], ['hnb%d' % k2, 'ssqin'], accum=stat[:, 0:1])
            rsqrt_col(rstdb[:, t_:t_ + 1], stat[:, 0:1], D, 'ssq')
            act(hnb[k2][:], h1[:, t_, :], AF.Copy, ['ssqrs'], ['hnb%d' % k2], scale=rstdb[:, t_:t_ + 1])

        def p3b_tb(t_):
            k2 = t_ % 2
            tg, tl = divmod(t_, 4)
            g2 = tg % 2
            b = 0
            for c in range(8):
                tr(psb[b][:, c * 128:(c + 1) * 128], hnb[k2][:, c * 128:(c + 1) * 128],
                   ['hnb%d' % k2], ['ps%d' % b])
            srcv = psb[b][:, 0:1024].rearrange("p (c t) -> p c t", c=8)
            tt('dve', hTb[g2][:, :, tl * 128:(tl + 1) * 128], srcv,
               gains[:, 16:24].unsqueeze(2).broadcast_to([128, 8, 128]), ALU.mult, ['ps%d' % b], ['hTb%d' % g2])
            tt('dve', hTkv[g2][:, :, tl * 128:(tl + 1) * 128], srcv,
               gains[:, 8:16].unsqueeze(2).broadcast_to([128, 8, 128]), ALU.mult, ['ps%d' % b], ['hTkv%d' % g2])

        def g0(tg):
            g2 = tg % 2
            for j in range(2):
                for c in range(8):
                    mm(ps[1 + j][:, 0:512], kvd[:, c, j * 128:(j + 1) * 128], hTkv[g2][:, c, :], c == 0, c == 7,
                       ['kvd', 'hTkv%d' % g2], ['ps%d' % (1 + j)])
            for c in range(8):
                mm(ps[3][0:48, 0:512], kvd[:, c, 256:304], hTkv[g2][:, c, :], c == 0, c == 7,
                   ['kvd', 'hTkv%d' % g2], ['ps3'])
            for j in range(2):
                act(sq[:, j, :], ps[1 + j][:, 0:512], AF.Square, ['ps%d' % (1 + j)], ['sq%d' % j])

        def g1(tg):
            gcols = slice(tg * 512, (tg + 1) * 512)
            for j in range(2):
                mm(ps[4][:, 0:512], ones, sq[:, j, :], j == 0, j == 1, ['sq%d' % j], ['ps4'])
            act(rsf[:], ps[4][:, 0:512], AF.Sqrt, ['ps4'], ['rsf'], scale=1.0 / 256, bias=EPS)
            recip(rsf[:], rsf[:], ['rsf'], ['rsf'])
            for j in range(2):
                stt('dve', ckvT[:, j, gcols], ps[1 + j][:, 0:512], gains[:, 24 + j:25 + j], rsf[:],
                    ALU.mult, ALU.mult, ['ps%d' % (1 + j), 'rsf'], ['ckvT'])
            kk = tg % 2
            tt('dve', t1s[kk][0:48, :], ps[3][0:48, 0:512], tabC[0:48, gcols], ALU.mult, ['ps3'], ['t1_%d' % kk])
            tt('dve', t2s[kk][0:16, :], ps[3][32:48, 0:512], tabS[0:16, gcols], ALU.mult, ['ps3'], ['t2_%da' % kk])
            tt('dve', t2s[kk][32:48, :], ps[3][0:16, 0:512], tabS[32:48, gcols], ALU.mult, ['ps3'], ['t2_%db' % kk])
            tt('pool', krT[0:48, gcols], t1s[kk][0:48, :], t2s[kk][0:48, :], ALU.add,
               ['t1_%d' % kk, 't2_%da' % kk, 't2_%db' % kk], ['krT'])

        def g2_(tg):
            g2 = tg % 2
            for j in range(3):
                for c in range(8):
                    mm(ps[5 + j][:, 0:512], wbq[:, c, j * 128:(j + 1) * 128], hTb[g2][:, c, :], c == 0, c == 7,
                       ['wbq', 'hTb%d' % g2], ['ps%d' % (5 + j)])
                act(sq[:, j, :], ps[5 + j][:, 0:512], AF.Square, ['ps%d' % (5 + j)], ['sq%d' % j])

        def g3(tg):
            gcols = slice(tg * 512, (tg + 1) * 512)
            for j in range(3):
                mm(ps[4][:, 0:512], ones, sq[:, j, :], j == 0, j == 2, ['sq%d' % j], ['ps4'])
            act(rsf[:], ps[4][:, 0:512], AF.Sqrt, ['ps4'], ['rsf'], scale=1.0 / 384, bias=EPS)
            recip(rsf[:], rsf[:], ['rsf'], ['rsf'])
            for j in range(3):
                stt('dve', cqT[:, j, gcols], ps[5 + j][:, 0:512], gains[:, 26 + j:27 + j], rsf[:],
                    ALU.mult, ALU.mult, ['ps%d' % (5 + j), 'rsf'], ['cqT'])

        GP = [g0, g1, g2_, g3]
        p3b_ta(0)
        for t_ in range(NT):
            if t_ + 1 < NT:
                p3b_ta(t_ + 1)
            p3b_tb(t_)
            tg, tl = divmod(t_, 4)
            if tg >= 1:
                GP[tl](tg - 1)
        for f in GP:
            f(3)
        S.barrier()

        obuf = sb("obuf", [128, NT, D], BF16, YT_OFF)
        loc = Bump(p4_start, LOC_END)
        qTb = [loc("qTb%d" % i, [128, T], BF16) for i in range(2)]
        kTb = [loc("kTb%d" % i, [128, T], BF16) for i in range(2)]
        vb = [loc("vb%d" % i, [128, 16, 66], BF16) for i in range(2)]
        pTb = [loc("pTb%d" % i, [128, 512], BF16) for i in range(4)]
        t1s = [loc("t1c%d" % i, [64, 512], F32) for i in range(2)]
        t2s = [loc("t2c%d" % i, [64, 512], F32) for i in range(2)]
        stg = ([loc("stgAc%d" % i, [64, 512], F32) for i in range(2)],
               [loc("stgBc%d" % i, [64, 512], F32) for i in range(2)])
        rl = [loc("rl%d" % i, [128, 4], F32) for i in range(2)]
        for k_ in range(2):
            memset('pool', vb[k_][:, :, 64:65], 1.0, ['vb%d' % k_])
        pj = Rot([0, 1])
        STB = [2, 3, 4, 7]
        OVB = [5, 6]
        cnt = {'rope': 0}

        def bq_item(h, tc):
            k = h % 2
            b = pj()
            cols = slice(tc * 512, (tc + 1) * 512)
            for c in range(3):
                mm(ps[b][0:96, 0:512], wqu[:, c, h, :], cqT[:, c, cols], c == 0, c == 2, [], ['ps%d' % b])
            rope(qTb[k], 'qTb%d' % k, b, tc, 1, 96, t1s, t2s, cnt['rope'], stg)
            cnt['rope'] += 1

        def bk_item(h, tc):
            k = h % 2
            b = pj()
            cols = slice(tc * 512, (tc + 1) * 512)
            for c in range(2):
                mm(ps[b][0:96, 0:512], kupk[:, c, h, :], ckvT[:, c, cols], c == 0, False, [], ['ps%d' % b])
            mm(ps[b][0:96, 0:512], ident[0:48, 0:96], krT[0:48, cols], False, True, [], ['ps%d' % b])
            cp('act', kTb[k][0:96, cols], ps[b][0:96, 0:512], ['ps%d' % b], ['kTb%d' % k])

        def bv_item(h, half):
            k = h % 2
            b = pj()
            for tl in range(8):
                t_ = half * 8 + tl
                for c in range(2):
                    mm(ps[b][:, tl * 64:(tl + 1) * 64], ckvT[:, c, t_ * 128:(t_ + 1) * 128], kupv[:, c, h, :],
                       c == 0, c == 1, [], ['ps%d' % b])
            cp('dve', vb[k][:, half * 8:(half + 1) * 8, 0:64],
               ps[b][:, 0:512].rearrange("p (a b) -> p a b", a=8), ['ps%d' % b], ['vb%d' % k])

        def projB_items(h):
            its = [(bq_item, (h, tc)) for tc in range(4)]
            its += [(bk_item, (h, tc)) for tc in range(4)]
            its += [(bv_item, (h, hf)) for hf in range(2)]
            return its

        ITS = [(qg, kb) for qg in range(4) for kb in range(4 * qg + 4)]

        def geom(qg, kb):
            jd = kb - 4 * qg
            q0 = (4 * qg + max(jd, 0)) * 128
            nq = (qg + 1) * 512 - q0
            return jd, q0, nq

        def b_qk(h, i):
            k = h % 2
            qg, kb = ITS[i]
            jd, q0, nq = geom(qg, kb)
            gi = h * 40 + i
            sb_ = STB[gi % 4]
            qn = ['qTb%dlo' % k, 'qTb%dhi' % k, 'kTb%d' % k]
            mm(ps[sb_][:, 0:nq], kTb[k][0:96, kb * 128:(kb + 1) * 128], qTb[k][0:96, q0:q0 + nq],
               True, jd < 0, qn, ['ps%d' % sb_])
            if jd >= 0:
                mm(ps[sb_][:, 0:128], ident, maskcur, False, True, [], ['ps%d' % sb_])

        def b_exp(h, i):
            qg, kb = ITS[i]
            jd, q0, nq = geom(qg, kb)
            gi = h * 40 + i
            sb_ = STB[gi % 4]
            pk = gi % 4
            pn = 'pTb%d' % pk
            act(pTb[pk][:, 0:nq], ps[sb_][:, 0:nq], AF.Exp, ['ps%d' % sb_], [pn], scale=SCALE_B)

        def b_pv(h, i):
            k = h % 2
            qg, kb = ITS[i]
            jd, q0, nq = geom(qg, kb)
            gi = h * 40 + i
            pk = gi % 4
            pn = 'pTb%d' % pk
            ob_ = OVB[(h * 4 + qg) % 2]
            on = 'ps%d' % ob_
            for ii in range(nq // 128):
                jq = max(jd, 0) + ii
                mm(ps[ob_][:, jq * 65:(jq + 1) * 65], pTb[pk][:, ii * 128:(ii + 1) * 128], vb[k][:, kb, 0:65],
                   kb == 0 and ii == 0, kb == 4 * qg + jq, [pn, 'vb%d' % k], [on])
            if kb == 4 * qg + 3:
                rk = qg % 2
                ov = ps[ob_][:, 0:260].rearrange("p (a b) -> p a b", a=4)
                recip(rl[rk][:], ov[:, :, 64], [on], ['rl%d' % rk])
                tt('dve', obuf[:, 4 * qg:4 * qg + 4, h * 64:(h + 1) * 64], ov[:, :, 0:64],
                   rl[rk][:].unsqueeze(2).broadcast_to([128, 4, 64]), ALU.mult, [on, 'rl%d' % rk], ['obuf'])

        for f, a in projB_items(0):
            f(*a)
        NI = len(ITS)
        for h in range(16):
            nxt = projB_items(h + 1) if h + 1 < 16 else []
            b_qk(h, 0)
            b_exp(h, 0)
            b_qk(h, 1)
            ni = 0
            for i in range(NI):
                if i + 2 < NI:
                    b_qk(h, i + 2)
                if i + 1 < NI:
                    b_exp(h, i + 1)
                b_pv(h, i)
                target = min(len(nxt), (len(nxt) * (i + 1) + 27) // 28)
                while ni < target:
                    f, a = nxt[ni]
                    f(*a)
                    ni += 1
        S.barrier()

        loc = Bump(LOC_OFF, LOC_END)
        wbz = loc("wbz", [128, 8, 1024], BF16)
        woutb = loc("woutb", [128, 8, 1024], BF16)
        hnb = [loc("hnbC%d" % i, [128, D], BF16) for i in range(2)]
        hTt = [loc("hTt%d" % i, [128, 8, 128], BF16) for i in range(2)]
        thz = [loc("thz%d" % i, [128, D], F32) for i in range(2)]
        ybt = [loc("ybt%d" % i, [128, D], BF16) for i in range(2)]
        yTt = [loc("yTt%d" % i, [128, 8, 128], BF16) for i in range(2)]
        junk = [loc("junkC%d" % i, [128, 512], BF16) for i in range(2)]
        tmpf = [loc("tmpfC%d" % i, [128, D], F32) for i in range(2)]
        ost = [loc("ost%d" % i, [128, D], F32) for i in range(2)]
        for c_ in range(8):
            dma('pool', wbz[:, c_, :], wbz_d[:, c_ * 1024:(c_ + 1) * 1024], 'wbz%d' % c_, (), ['wbz%d' % c_])
        for c_ in range(8):
            dma('pool', woutb[:, c_, :], woutb_d[:, c_ * 1024:(c_ + 1) * 1024], 'wout%d' % c_, (), ['wout%d' % c_])
        TA = [4, 5]
        TB = [6, 7]

        def p5_a1(t_):
            k2 = t_ % 2
            ts('dve', hnb[k2][:], h1[:, t_, :], rstdb[:, t_:t_ + 1], None, ALU.mult, None, [], ['hnb%d' % k2])

        def p5_a2(t_):
            k2 = t_ % 2
            ba = TA[k2]
            for c in range(8):
                tr(psb[ba][:, c * 128:(c + 1) * 128], hnb[k2][:, c * 128:(c + 1) * 128], ['hnb%d' % k2],
                   ['ps%d' % ba])
            tt('dve', hTt[k2][:], psb[ba][:, 0:1024].rearrange("p (c t) -> p c t", c=8),
               gains[:, 16:24].unsqueeze(2).broadcast_to([128, 8, 128]), ALU.mult, ['ps%d' % ba], ['hTt%d' % k2])

        def p5_b(t_):
            k2 = t_ % 2
            for hf in range(2):
                b = [[0, 1], [2, 3]][k2][hf]
                hc = slice(hf * 512, (hf + 1) * 512)
                for c in range(8):
                    mm(ps[b][:, 0:512], hTt[k2][:, c, :], wbz[:, c, hc], c == 0, c == 7,
                       ['hTt%d' % k2, 'wbz%d' % c], ['ps%d' % b])
                act(thz[k2][:, hc], ps[b][:, 0:512], AF.Tanh, ['ps%d' % b], ['thz%d' % k2], scale=0.5)
                stt('dve', thz[k2][:, hc], thz[k2][:, hc], 1.0, ps[b][:, 0:512], ALU.add, ALU.mult,
                    ['thz%d' % k2, 'ps%d' % b], ['thz%d' % k2])
            stt('dve', obuf[:, t_, :], thz[k2][:], 0.5, obuf[:, t_, :], ALU.mult, ALU.mult, ['thz%d' % k2],
                ['ob%d' % t_])

        def p5_c(t_):
            k2 = t_ % 2
            bb = TB[k2]
            for c in range(8):
                tr(psb[bb][:, c * 128:(c + 1) * 128], obuf[:, t_, c * 128:(c + 1) * 128], ['ob%d' % t_],
                   ['ps%d' % bb])
            cp('act', yTt[k2][:], psb[bb][:, 0:1024].rearrange("p (c t) -> p c t", c=8), ['ps%d' % bb],
               ['yTt%d' % k2])

        def p5_d(t_):
            k2 = t_ % 2
            for hf in range(2):
                b = [[0, 1], [2, 3]][k2][hf]
                for c in range(8):
                    mm(ps[b][:, 0:512], yTt[k2][:, c, :], woutb[:, c, hf * 512:(hf + 1) * 512], c == 0, c == 7,
                       ['wout%d' % c, 'yTt%d' % k2], ['ps%d' % b])

        def p5_e(t_):
            k2 = t_ % 2
            b0, b1 = [[0, 1], [2, 3]][k2]
            c0_ = 16 + 4 * k2
            act(junk[0][:], ps[b0][:, 0:512], AF.Square, ['ps%d' % b0], ['junk0', 'pq0'], accum=stat[:, c0_:c0_ + 1])
            act(junk[1][:], ps[b1][:, 0:512], AF.Square, ['ps%d' % b1], ['junk1', 'pq1'],
                accum=stat[:, c0_ + 1:c0_ + 2])
            tt('dve', stat[:, c0_ + 2:c0_ + 3], stat[:, c0_:c0_ + 1], stat[:, c0_ + 1:c0_ + 2], ALU.add,
               ['pq0', 'pq1'], ['pnin'])
            rsqrt_col(stat[:, c0_ + 3:c0_ + 4], stat[:, c0_ + 2:c0_ + 3], D, 'pn')
            for hf, b in ((0, b0), (1, b1)):
                stt('dve', tmpf[k2][:, hf * 512:(hf + 1) * 512], ps[b][:, 0:512], stat[:, c0_ + 3:c0_ + 4],
                    gpost[:, 1, hf * 512:(hf + 1) * 512], ALU.mult, ALU.mult, ['ps%d' % b, 'pnrs'],
                    ['tmpf%d' % k2])
            tt('pool', ost[k2][:], h1[:, t_, :], tmpf[k2][:], ALU.add, ['tmpf%d' % k2], ['ost%d' % k2])
            dma('sp', out[s, t_ * 128:(t_ + 1) * 128, :], ost[k2][:], 'ost%d' % k2, ['ost%d' % k2],
                ['out%d_%d' % (s, t_)])

        tmps = tab_tmps(loc)

        def p5_t(t_):
            if s + 1 < NSEQ and t_ % 4 == 1:
                tab_prep(s + 1, 29, t_ // 4, tmps)

        pipeline([p5_a1, p5_a2, p5_b], NT, newest_first=True)
        pipeline([p5_c, p5_d, p5_e, p5_t], NT, newest_first=True)
        if s + 1 < NSEQ:
            tab_sin()
        S.barrier()

    S.emit()
    return nc


def _chunked(w):
    K, N = w.shape
    return np.ascontiguousarray(w.reshape(K // 128, 128, N).transpose(1, 0, 2))


def _prep_weights(inp):
    f32 = np.float32
    perm = np.concatenate([np.arange(0, 16), np.arange(32, 48), np.arange(16, 32), np.arange(48, 64),
                           np.arange(64, 128)])
    w_in = np.asarray(inp['a_w_in'][0], f32)
    wqkv = np.empty((24, 128, 8, 384), f32)
    for hs in range(8):
        for g in range(3):
            u = hs * 3 + g
            for j in range(3):
                c0 = g * 3072 + j * 1024 + hs * 128
                blk = w_in[:, c0:c0 + 128]
                if j < 2:
                    blk = blk[:, perm]
                wqkv[u, :, :, j * 128:(j + 1) * 128] = _chunked(blk)
    wz = np.empty((8, 128, 8, 128), f32)
    for hs in range(8):
        wz[hs] = _chunked(w_in[:, 9216 + hs * 128:9216 + (hs + 1) * 128])
    wouta = _chunked(np.asarray(inp['a_w_out'][0], f32))
    kvdw = np.asarray(inp['kv_w_down'], f32)
    kvd = np.zeros((1024, 304), f32)
    kvd[:, 0:256] = kvdw[:, 0:256]
    kvd[:, 256:272] = kvdw[:, 256:272]
    kvd[:, 288:304] = kvdw[:, 272:288]
    kvd = _chunked(kvd)
    kup = np.asarray(inp['kv_w_up'], f32).reshape(256, 16, 128)
    kupk = np.zeros((256, 16, 96), f32)
    kupk[:, :, 16:32] = kup[:, :, 0:16]
    kupk[:, :, 48:96] = kup[:, :, 16:64]
    kupk = _chunked(kupk.reshape(256, 16 * 96))
    kupv = _chunked(np.ascontiguousarray(kup[:, :, 64:128]).reshape(256, 16 * 64))
    bw_in = np.asarray(inp['b_w_in'][0], f32)
    wbq = _chunked(np.ascontiguousarray(bw_in[:, 0:384]))
    wbz = _chunked(np.ascontiguousarray(bw_in[:, 384:1408]))
    qu = np.asarray(inp['b_w_q_up'][0], f32).reshape(384, 16, 96)
    wqu = np.empty((384, 16, 96), f32)
    wqu[:, :, 0:16] = qu[:, :, 64:80]
    wqu[:, :, 16:32] = qu[:, :, 0:16]
    wqu[:, :, 32:48] = qu[:, :, 80:96]
    wqu[:, :, 48:96] = qu[:, :, 16:64]
    wqu = _chunked(wqu.reshape(384, 16 * 96))
    woutb = _chunked(np.asarray(inp['b_w_out'][0], f32))

    gains = np.zeros((128, 40), f32)
    gains[:, 0:8] = np.asarray(inp['a_pre_norm'][0], f32).reshape(8, 128).T
    gains[:, 8:16] = np.asarray(inp['kv_norm'], f32).reshape(8, 128).T
    gains[:, 16:24] = np.asarray(inp['b_pre_norm'][0], f32).reshape(8, 128).T
    gains[:, 24:26] = np.asarray(inp['kv_latent_norm'], f32).reshape(2, 128).T
    gains[:, 26:29] = np.asarray(inp['b_q_norm'][0], f32).reshape(3, 128).T
    i16 = np.arange(16, dtype=np.float64)
    for col, theta in ((29, 500000.0), (32, 10000.0)):
        invf = 1.0 / (theta ** (i16 * (2.0 / 32)))
        fu = np.zeros(64)
        fu[0:16] = invf / (2 * np.pi)
        fu[32:48] = invf / (2 * np.pi)
        gains[0:64, col] = fu.astype(f32)
        gains[64:128, col] = fu.astype(f32)
    gains[:, 30] = 0.25
    gains[0:16, 31] = 0.5
    gains[64:80, 31] = 0.5
    gpost = np.stack([np.asarray(inp['a_post_norm'][0], f32), np.asarray(inp['b_post_norm'][0], f32)])
    kk = np.arange(128)[:, None]
    qq = np.arange(128)[None, :]
    cst = np.zeros((128, 768), f32)
    cst[:, 512:640] = (kk >= qq)
    cst[:, 640:768] = (kk <= qq)
    cst[:, 0:128] = np.eye(128, dtype=f32)
    cst[:, 128:256] = np.where(kk >= qq, 0.0, -30000.0)
    cst[:, 256:384] = np.where(kk <= qq, 0.0, -30000.0)
    cst[:, 384:512] = 1.0
    return dict(
        wqkv=np.ascontiguousarray(wqkv.reshape(24, 128, 8 * 384)),
        wz=np.ascontiguousarray(wz.reshape(8, 128, 8 * 128)),
        wouta=wouta.reshape(128, -1), kvd=kvd.reshape(128, -1), kupk=kupk.reshape(128, -1),
        kupv=kupv.reshape(128, -1), wbq=wbq.reshape(128, -1), wbz=wbz.reshape(128, -1),
        wqu=wqu.reshape(128, -1), woutb=woutb.reshape(128, -1), gains=gains, gpost=gpost, cst=cst)


def kernel(**inputs):
    x = np.asarray(inputs['x'], np.float32)
    positions = np.asarray(inputs['positions'], np.int32)
    B = x.shape[0]
    nseq = B // NCORES
    w = _prep_weights(inputs)
    w = {k: np.ascontiguousarray(v, dtype=np.float32) for k, v in w.items()}
    nc = build_program(nseq)
    in_maps = []
    for c in range(NCORES):
        m = dict(w)
        m['x'] = np.ascontiguousarray(x[c * nseq:(c + 1) * nseq])
        m['pos'] = np.ascontiguousarray(positions[c * nseq:(c + 1) * nseq])
        in_maps.append(m)
    res = run_bass_kernel_spmd(nc, in_maps, core_ids=list(range(NCORES)))
    return np.concatenate([np.asarray(r['out'], np.float32) for r in res.results], axis=0)
```
